# Optimizing a Trainium2 kernel written in Bass

```python
import math
import jax
import jax.numpy as jnp
from jax import lax
import numpy as np

D_MODEL = 2048
BATCH = 4
SEQ = 4096
DEPTH = 4

GRID_W = 64
CTX_LEN = 256
EPS = 1e-6
N_MOD = 6

ATTN_HEADS = 8
ATTN_KV_HEADS = 2
ATTN_REP = ATTN_HEADS // ATTN_KV_HEADS
HEAD_DIM = 128
ATTN_WIDTH = ATTN_HEADS * HEAD_DIM
KV_WIDTH = ATTN_KV_HEADS * HEAD_DIM
ROPE_AXIS_DIM = HEAD_DIM // 2
ROPE_FREQS = ROPE_AXIS_DIM // 2
ROPE_THETA = 10000.0
Q_BLOCK = 128
ATTN_SCALE = HEAD_DIM ** -0.5

CONF_WIDTH = 1024
CONF_KERNEL = 31

SSM_HEADS = 16
SSM_HEAD_DIM = 64
SSM_WIDTH = SSM_HEADS * SSM_HEAD_DIM
SSM_GROUPS = 2
SSM_HEADS_PER_GROUP = SSM_HEADS // SSM_GROUPS
SSM_STATE = 128
SSM_XBC = SSM_WIDTH + 2 * SSM_GROUPS * SSM_STATE
SSM_CONV = 5
SSM_CHUNK = 128

D_FF = 5632
FFN_CONV = 3

N_BRANCHES = 3
Q_END = ATTN_WIDTH
K_END = Q_END + KV_WIDTH
V_END = K_END + KV_WIDTH
CONF_END = V_END + 2 * CONF_WIDTH
Z_END = CONF_END + SSM_WIDTH
XBC_END = Z_END + SSM_XBC
DT_END = XBC_END + 2 * SSM_HEADS
IN_WIDTH = DT_END + N_BRANCHES * D_MODEL
IN_SPLITS = (Q_END, K_END, V_END, CONF_END, Z_END, XBC_END, DT_END)

kernel_name = "hybrid_gated_attn_conformer_ssd_dit"


def rmsnorm(x, w):
    x32 = x.astype(jnp.float32)
    y = x32 * lax.rsqrt(jnp.mean(jnp.square(x32), axis=-1, keepdims=True) + EPS)
    return (y * w.astype(jnp.float32)).astype(x.dtype)


def layernorm(x, w, b):
    x32 = x.astype(jnp.float32)
    mu = jnp.mean(x32, axis=-1, keepdims=True)
    var = jnp.mean(jnp.square(x32 - mu), axis=-1, keepdims=True)
    y = (x32 - mu) * lax.rsqrt(var + EPS) * w.astype(jnp.float32) + b.astype(jnp.float32)
    return y.astype(x.dtype)


def modulate(h, shift, scale):
    return h * (1 + scale) + shift


def depthwise_conv(x, w, b):
    k, ch = w.shape
    pad = (k - 1) // 2
    y = lax.conv_general_dilated(x, w[:, None, :], window_strides=(1,), padding=[(pad, pad)],
                                 dimension_numbers=('NWC', 'WIO', 'NWC'), feature_group_count=ch)
    return y + b


def axial_rope_tables(n_tokens):
    rows = n_tokens // GRID_W
    row = jnp.repeat(jnp.arange(rows), GRID_W).astype(jnp.float32)
    col = jnp.tile(jnp.arange(GRID_W), rows).astype(jnp.float32)
    inv = ROPE_THETA ** (-jnp.arange(0, ROPE_AXIS_DIM, 2, dtype=jnp.float32) / ROPE_AXIS_DIM)
    ang = jnp.stack([row[:, None] * inv, col[:, None] * inv], axis=1)
    return jnp.cos(ang), jnp.sin(ang)


def apply_rope(x, cos, sin):
    xr = x.reshape(*x.shape[:-1], 2, 2, ROPE_FREQS)
    x1, x2 = xr[..., 0, :], xr[..., 1, :]
    c, s = cos[None, :, None], sin[None, :, None]
    out = jnp.stack([x1 * c - x2 * s, x2 * c + x1 * s], axis=-2)
    return out.reshape(x.shape).astype(x.dtype)


def attend_blocks(q, k, v):
    b, lq = q.shape[:2]
    nb = lq // Q_BLOCK
    qb = q.reshape(b, nb, Q_BLOCK, ATTN_KV_HEADS, ATTN_REP, HEAD_DIM).swapaxes(0, 1)

    def one_block(qi):
        s = jnp.einsum('bqgrd,bkgd->bgrqk', qi, k, preferred_element_type=jnp.float32) * ATTN_SCALE
        p = jax.nn.softmax(s, axis=-1)
        return jnp.einsum('bgrqk,bkgd->bqgrd', p.astype(v.dtype), v)

    o = lax.map(one_block, qb)
    return o.swapaxes(0, 1).reshape(b, lq, ATTN_WIDTH)


def conformer_conv(u, conv_w, conv_b, ln_w, ln_b, w_o):
    a, g = jnp.split(u, 2, axis=-1)
    v = depthwise_conv(a * jax.nn.sigmoid(g), conv_w, conv_b)
    v = jax.nn.silu(layernorm(v, ln_w, ln_b))
    return v @ w_o


def ssd_inputs(xbc, dt_raw, conv_w, conv_b, dt_bias):
    b, n = xbc.shape[:2]
    xbc = jax.nn.silu(depthwise_conv(xbc, conv_w, conv_b))
    xs, bm, cm = jnp.split(xbc, [SSM_WIDTH, SSM_WIDTH + SSM_GROUPS * SSM_STATE], axis=-1)
    xs = xs.reshape(b, n, SSM_HEADS, SSM_HEAD_DIM)
    bm = bm.reshape(b, n, SSM_GROUPS, SSM_STATE)
    cm = cm.reshape(b, n, SSM_GROUPS, SSM_STATE)
    dt = jax.nn.softplus(dt_raw.astype(jnp.float32).reshape(b, n, 2, SSM_HEADS) + dt_bias.astype(jnp.float32))
    return xs, bm, cm, dt


def ssd_scan(x, dt, a_coef, bm, cm, h0, with_output):
    b, n = x.shape[:2]
    nc = n // SSM_CHUNK
    shp = (b, nc, SSM_CHUNK, SSM_GROUPS, SSM_HEADS_PER_GROUP)
    xc = x.reshape(*shp, SSM_HEAD_DIM)
    bc = bm.reshape(b, nc, SSM_CHUNK, SSM_GROUPS, SSM_STATE)
    cc = cm.reshape(b, nc, SSM_CHUNK, SSM_GROUPS, SSM_STATE)
    dtc = dt.reshape(shp)
    acum = jnp.cumsum(dtc * a_coef.reshape(SSM_GROUPS, SSM_HEADS_PER_GROUP), axis=2)
    xdt = xc * dtc[..., None]
    decay_to_end = jnp.exp(acum[:, :, -1:] - acum)
    states = jnp.einsum('bcjgn,bcjgrp->bcgrpn', bc, xdt * decay_to_end[..., None])
    chunk_decay = jnp.exp(acum[:, :, -1])

    def step(h, inp):
        s, dcy = inp
        return h * dcy[..., None, None] + s, h

    h_final, h_in = lax.scan(step, h0, (states.swapaxes(0, 1), chunk_decay.swapaxes(0, 1)))
    if not with_output:
        return None, h_final
    h_in = h_in.swapaxes(0, 1)
    seg = acum[:, :, :, None] - acum[:, :, None, :]
    lower = jnp.tril(jnp.ones((SSM_CHUNK, SSM_CHUNK), dtype=bool))[:, :, None, None]
    decay = jnp.exp(jnp.where(lower, seg, -jnp.inf))
    cb = jnp.einsum('bcign,bcjgn->bcijg', cc, bc)
    y_intra = jnp.einsum('bcijgr,bcjgrp->bcigrp', cb[..., None] * decay, xdt)
    y_inter = jnp.einsum('bcign,bcgrpn->bcigrp', cc, h_in) * jnp.exp(acum)[..., None]
    y = (y_intra + y_inter).reshape(b, n, SSM_HEADS, SSM_HEAD_DIM)
    return y, h_final


def ssd_mixer(xbc_lat, dt_lat, z_lat, xbc_ctx, dt_ctx, z_ctx, conv_w, conv_b, a_log, dt_bias, d_skip,
              norm_w, w_o, need_ctx_out):
    xs_l, b_l, c_l, dt_l = ssd_inputs(xbc_lat, dt_lat, conv_w, conv_b, dt_bias)
    xs_c, b_c, c_c, dt_c = ssd_inputs(xbc_ctx, dt_ctx, conv_w, conv_b, dt_bias)
    a_coef = -jnp.exp(a_log.astype(jnp.float32))
    bsz = xs_l.shape[0]
    h0 = jnp.zeros((bsz, SSM_GROUPS, SSM_HEADS_PER_GROUP, SSM_HEAD_DIM, SSM_STATE), jnp.float32)

    def rev(t):
        return jnp.flip(t, axis=1)

    y_cf, h_cf = ssd_scan(xs_c, dt_c[:, :, 0], a_coef[0], b_c, c_c, h0, need_ctx_out)
    y_lf, _ = ssd_scan(xs_l, dt_l[:, :, 0], a_coef[0], b_l, c_l, h_cf, True)
    y_cb, h_cb = ssd_scan(rev(xs_c), rev(dt_c[:, :, 1]), a_coef[1], rev(b_c), rev(c_c), h0, need_ctx_out)
    y_lb, _ = ssd_scan(rev(xs_l), rev(dt_l[:, :, 1]), a_coef[1], rev(b_l), rev(c_l), h_cb, True)
    d_sum = (d_skip[0] + d_skip[1])[:, None]

    def gated_out(y, xs, z):
        y = (y + d_sum * xs).astype(z.dtype).reshape(*z.shape[:2], SSM_WIDTH)
        return rmsnorm(y * jax.nn.silu(z), norm_w) @ w_o

    out_lat = gated_out(y_lf + rev(y_lb), xs_l, z_lat)
    out_ctx = gated_out(y_cf + rev(y_cb), xs_c, z_ctx) if need_ctx_out else None
    return out_lat, out_ctx


def token_mixer(h_lat, h_ctx, w_in, q_norm_w, k_norm_w, w_attn_o, conf_conv_w, conf_conv_b, conf_ln_w,
                conf_ln_b, w_conf_o, ssm_conv_w, ssm_conv_b, ssm_a_log, ssm_dt_bias, ssm_d, ssm_norm_w,
                w_ssm_o, w_out, rope_cos, rope_sin, need_ctx_out):
    bsz, n_lat = h_lat.shape[:2]
    n_ctx = h_ctx.shape[1]
    q_l, k_l, v_l, conf_l, z_l, xbc_l, dt_l, gate_l = jnp.split(h_lat @ w_in, IN_SPLITS, axis=-1)
    q_c, k_c, v_c, conf_c, z_c, xbc_c, dt_c, gate_c = jnp.split(h_ctx @ w_in, IN_SPLITS, axis=-1)

    q_l = apply_rope(rmsnorm(q_l.reshape(bsz, n_lat, ATTN_HEADS, HEAD_DIM), q_norm_w), rope_cos, rope_sin)
    k_l = apply_rope(rmsnorm(k_l.reshape(bsz, n_lat, ATTN_KV_HEADS, HEAD_DIM), k_norm_w), rope_cos, rope_sin)
    v_l = v_l.reshape(bsz, n_lat, ATTN_KV_HEADS, HEAD_DIM)
    q_c = rmsnorm(q_c.reshape(bsz, n_ctx, ATTN_HEADS, HEAD_DIM), q_norm_w)
    k_c = rmsnorm(k_c.reshape(bsz, n_ctx, ATTN_KV_HEADS, HEAD_DIM), k_norm_w)
    v_c = v_c.reshape(bsz, n_ctx, ATTN_KV_HEADS, HEAD_DIM)
    k_all = jnp.concatenate([k_c, k_l], axis=1)
    v_all = jnp.concatenate([v_c, v_l], axis=1)
    q_l = q_l.reshape(bsz, n_lat, ATTN_KV_HEADS, ATTN_REP, HEAD_DIM)
    ya_l = attend_blocks(q_l, k_all, v_all) @ w_attn_o

    yb_l = conformer_conv(conf_l, conf_conv_w, conf_conv_b, conf_ln_w, conf_ln_b, w_conf_o)

    yc_l, yc_c = ssd_mixer(xbc_l, dt_l, z_l, xbc_c, dt_c, z_c, ssm_conv_w, ssm_conv_b, ssm_a_log, ssm_dt_bias,
                           ssm_d, ssm_norm_w, w_ssm_o, need_ctx_out)

    def merge(ya, yb, yc, gates):
        g = jax.nn.sigmoid(gates.reshape(*gates.shape[:2], N_BRANCHES, D_MODEL))
        return (g[:, :, 0] * ya + g[:, :, 1] * yb + g[:, :, 2] * yc) @ w_out

    out_lat = merge(ya_l, yb_l, yc_l, gate_l)
    if not need_ctx_out:
        return out_lat, None
    q_c = q_c.reshape(bsz, n_ctx, ATTN_KV_HEADS, ATTN_REP, HEAD_DIM)
    ya_c = attend_blocks(q_c, k_c, v_c) @ w_attn_o
    yb_c = conformer_conv(conf_c, conf_conv_w, conf_conv_b, conf_ln_w, conf_ln_b, w_conf_o)
    out_ctx = merge(ya_c, yb_c, yc_c, gate_c)
    return out_lat, out_ctx


def conv_ffn(h, w_up, conv_w, conv_b, w_down):
    u = depthwise_conv(h @ w_up, conv_w, conv_b)
    g, v = jnp.split(u, 2, axis=-1)
    return (jax.nn.silu(g) * v) @ w_down


def setup_inputs(seed: int = 0) -> dict:
    key = jax.random.key(seed)
    ks = iter(jax.random.split(key, 48))

    def nrm(shape, scale):
        return jax.random.normal(next(ks), shape, jnp.float32) * scale

    def gain(shape):
        return 1.0 + nrm(shape, 0.02)

    L = DEPTH
    dt0 = jnp.exp(jax.random.uniform(next(ks), (L, 2, SSM_HEADS), jnp.float32,
                                     minval=math.log(1e-3), maxval=math.log(1e-1)))
    dt_bias = dt0 + jnp.log(-jnp.expm1(-dt0))
    a_log = jnp.log(jax.random.uniform(next(ks), (L, 2, SSM_HEADS), jnp.float32, minval=1.0, maxval=16.0))
    return {
        'x': nrm((BATCH, SEQ, D_MODEL), 1.0),
        'c': nrm((BATCH, D_MODEL), 1.0),
        'ctx': nrm((BATCH, CTX_LEN, D_MODEL), 1.0),
        'c_ctx': nrm((D_MODEL,), 1.0),
        'w_mod': nrm((L, D_MODEL, N_MOD * D_MODEL), 0.5 * D_MODEL ** -0.5),
        'b_mod': nrm((L, N_MOD * D_MODEL), 0.01),
        'norm_mix_w': gain((L, D_MODEL)),
        'norm_ffn_w': gain((L, D_MODEL)),
        'w_in': nrm((L, D_MODEL, IN_WIDTH), D_MODEL ** -0.5),
        'q_norm_w': gain((L, HEAD_DIM)),
        'k_norm_w': gain((L, HEAD_DIM)),
        'w_attn_o': nrm((L, ATTN_WIDTH, D_MODEL), ATTN_WIDTH ** -0.5),
        'conf_conv_w': nrm((L, CONF_KERNEL, CONF_WIDTH), CONF_KERNEL ** -0.5),
        'conf_conv_b': nrm((L, CONF_WIDTH), 0.01),
        'conf_ln_w': gain((L, CONF_WIDTH)),
        'conf_ln_b': nrm((L, CONF_WIDTH), 0.01),
        'w_conf_o': nrm((L, CONF_WIDTH, D_MODEL), CONF_WIDTH ** -0.5),
        'ssm_conv_w': nrm((L, SSM_CONV, SSM_XBC), SSM_CONV ** -0.5),
        'ssm_conv_b': nrm((L, SSM_XBC), 0.01),
        'ssm_a_log': a_log,
        'ssm_dt_bias': dt_bias,
        'ssm_d': gain((L, 2, SSM_HEADS)),
        'ssm_norm_w': gain((L, SSM_WIDTH)),
        'w_ssm_o': nrm((L, SSM_WIDTH, D_MODEL), SSM_WIDTH ** -0.5),
        'w_out': nrm((L, D_MODEL, D_MODEL), D_MODEL ** -0.5),
        'ffn_w_up': nrm((L, D_MODEL, 2 * D_FF), D_MODEL ** -0.5),
        'ffn_conv_w': nrm((L, FFN_CONV, 2 * D_FF), FFN_CONV ** -0.5),
        'ffn_conv_b': nrm((L, 2 * D_FF), 0.01),
        'ffn_w_down': nrm((L, D_FF, D_MODEL), D_FF ** -0.5),
        'final_norm_w': gain((D_MODEL,)),
    }


def reference(x, c, ctx, c_ctx, w_mod, b_mod, norm_mix_w, norm_ffn_w, w_in, q_norm_w, k_norm_w, w_attn_o,
              conf_conv_w, conf_conv_b, conf_ln_w, conf_ln_b, w_conf_o, ssm_conv_w, ssm_conv_b, ssm_a_log,
              ssm_dt_bias, ssm_d, ssm_norm_w, w_ssm_o, w_out, ffn_w_up, ffn_conv_w, ffn_conv_b, ffn_w_down,
              final_norm_w):
    bsz = x.shape[0]
    rope_cos, rope_sin = axial_rope_tables(x.shape[1])
    for l in range(DEPTH):
        need_ctx = l < DEPTH - 1
        m_lat = (jax.nn.silu(c) @ w_mod[l] + b_mod[l]).reshape(bsz, 1, N_MOD, D_MODEL)
        m_ctx = (jax.nn.silu(c_ctx) @ w_mod[l] + b_mod[l]).reshape(1, 1, N_MOD, D_MODEL)
        h_lat = modulate(rmsnorm(x, norm_mix_w[l]), m_lat[:, :, 0], m_lat[:, :, 1])
        h_ctx = modulate(rmsnorm(ctx, norm_mix_w[l]), m_ctx[:, :, 0], m_ctx[:, :, 1])
        y_lat, y_ctx = token_mixer(h_lat, h_ctx, w_in[l], q_norm_w[l], k_norm_w[l], w_attn_o[l], conf_conv_w[l],
                                   conf_conv_b[l], conf_ln_w[l], conf_ln_b[l], w_conf_o[l], ssm_conv_w[l],
                                   ssm_conv_b[l], ssm_a_log[l], ssm_dt_bias[l], ssm_d[l], ssm_norm_w[l],
                                   w_ssm_o[l], w_out[l], rope_cos, rope_sin, need_ctx)
        x = x + m_lat[:, :, 2] * y_lat
        h_lat = modulate(rmsnorm(x, norm_ffn_w[l]), m_lat[:, :, 3], m_lat[:, :, 4])
        x = x + m_lat[:, :, 5] * conv_ffn(h_lat, ffn_w_up[l], ffn_conv_w[l], ffn_conv_b[l], ffn_w_down[l])
        if need_ctx:
            ctx = ctx + m_ctx[:, :, 2] * y_ctx
            h_ctx = modulate(rmsnorm(ctx, norm_ffn_w[l]), m_ctx[:, :, 3], m_ctx[:, :, 4])
            ctx = ctx + m_ctx[:, :, 5] * conv_ffn(h_ctx, ffn_w_up[l], ffn_conv_w[l], ffn_conv_b[l], ffn_w_down[l])
    return rmsnorm(x, final_norm_w)
```

```python
import math
from contextlib import ExitStack
import numpy as np
import concourse.bass as bass
import concourse.mybir as mybir
from concourse.bass_utils import run_bass_kernel_spmd

F32 = mybir.dt.float32
BF16 = mybir.dt.bfloat16
AF = mybir.ActivationFunctionType
ALU = mybir.AluOpType

D = 2048
NFC = 16
CTX = 256
DEPTH = 4
EPS = 1e-6
IN_W = 12320
DFF = 5632
ATT_SCALE = 128 ** -0.5


class Buf:
    __slots__ = ("name", "t", "last_w", "reads", "dsem", "dcnt")

    def __init__(self, name, t=None):
        self.name = name
        self.t = t
        self.last_w = None
        self.reads = {}
        self.dsem = None
        self.dcnt = 0

    def __getitem__(self, idx):
        return self.t[idx]


class Eng:
    def __init__(self, fw, name, eng, counted=True):
        self.name = name
        self.eng = eng
        self.sem = fw.nc.alloc_semaphore("sem_" + name) if counted else None
        self.count = 0
        self.seen = {}


class FW:
    def __init__(self, nc):
        self.nc = nc
        self.pe = Eng(self, "pe", nc.tensor)
        self.act = Eng(self, "act", nc.scalar)
        self.dve = Eng(self, "dve", nc.vector)
        self.pool = Eng(self, "pool", nc.gpsimd)
        self.sp = Eng(self, "sp", nc.sync, counted=False)
        self.engs = [self.pe, self.act, self.dve, self.pool, self.sp]
        self.dsems = []
        self.free_dsems = []
        self.n_wait = 0
        self.n_ins = 0
        self.stack = None
        self.uid = 0
        self.live = [[]]

    def sb(self, name, shape, dtype):
        self.uid += 1
        name = "%s_%d" % (name, self.uid)
        b = Buf(name, self.stack.enter_context(self.nc.sbuf_tensor(name, list(shape), dtype)))
        self.live[-1].append(b)
        return b

    def ps(self, name, shape, dtype=F32):
        self.uid += 1
        name = "%s_%d" % (name, self.uid)
        return Buf(name, self.stack.enter_context(self.nc.psum_tensor(name, list(shape), dtype)))

    def dram(self, name, shape, dtype, kind="Internal"):
        return Buf(name, self.nc.dram_tensor(name, list(shape), dtype, kind=kind))

    def _wait(self, E, tok):
        sem, val = tok
        k = id(sem)
        if E.seen.get(k, 0) >= val:
            return
        if sem is E.sem and E.name == "pe":
            return
        E.eng.wait_ge(sem, val)
        E.seen[k] = val
        self.n_wait += 1

    def _deps(self, E, reads, writes):
        for b in reads:
            if b.last_w is not None:
                self._wait(E, b.last_w)
        for b in writes:
            if b.last_w is not None:
                self._wait(E, b.last_w)
            for tok in b.reads.values():
                self._wait(E, tok)

    def _commit(self, tok, reads, writes):
        k = id(tok[0])
        for b in reads:
            b.reads[k] = tok
        for b in writes:
            b.last_w = tok
            b.reads = {}

    def op(self, E, fn, reads=(), writes=()):
        self._deps(E, reads, writes)
        ins = fn(E.eng)
        E.count += 1
        ins.then_inc(E.sem, 1)
        self._commit((E.sem, E.count), reads, writes)
        self.n_ins += 1
        return ins

    def dma(self, Q, out_ap, in_ap, sb, reads=(), writes=()):
        if sb.dsem is None:
            if self.free_dsems:
                sb.dsem = self.free_dsems.pop()
            else:
                sb.dsem = [self.nc.alloc_semaphore("ds_%d" % len(self.dsems)), 0]
                self.dsems.append(sb.dsem)
        ds = sb.dsem
        if sb.dcnt:
            self._wait(Q, (ds[0], 16 * ds[1]))
        self._deps(Q, reads, writes)
        ins = Q.eng.dma_start(out=out_ap, in_=in_ap)
        sb.dcnt += 1
        ds[1] += 1
        ins.then_inc(ds[0], 16)
        self._commit((ds[0], 16 * ds[1]), reads, writes)
        self.n_ins += 1
        return ins

    def barrier(self):
        toks = [(e.sem, e.count) for e in self.engs if e.sem is not None and e.count]
        toks += [(d[0], 16 * d[1]) for d in self.dsems if d[1]]
        for E in self.engs:
            for tok in toks:
                self._wait(E, tok)


class Builder:
    def __init__(self, L=4096, depth=DEPTH, debug=False):
        self.L = L
        self.T = CTX + L
        self.NCH = self.T // 128
        self.depth = depth
        self.debug = debug
        self.tiles = [(0, CTX, 1)] + [(CTX + 512 * i, 512, 0) for i in range(L // 512)]
        self.nc = bass.Bass("TRN2", target_bir_lowering=False)
        self.fw = FW(self.nc)

    def inp(self, name, shape, dtype=F32):
        return self.fw.dram(name, shape, dtype, kind="ExternalInput")

    def scratch(self, name, shape, dtype):
        return self.fw.dram(name, shape, dtype, kind="ExternalOutput" if self.debug else "Internal")

    def phase(self):
        b = self

        class _P:
            def __enter__(s):
                s.prev = b.fw.stack
                s.st = ExitStack()
                s.st.__enter__()
                b.fw.stack = s.st
                b.fw.live.append([])
                return s

            def __exit__(s, *a):
                b.fw.barrier()
                for bf in b.fw.live.pop():
                    if bf.dsem is not None:
                        b.fw.free_dsems.append(bf.dsem)
                        bf.dsem = None
                s.st.__exit__(*a)
                b.fw.stack = s.prev
                return False
        return _P()

    def build(self):
        fw, nc, L, T = self.fw, self.nc, self.L, self.T
        dp = self.depth
        self.x_in = self.inp("x", [L, D])
        self.ctx_in = self.inp("ctx", [CTX, D])
        self.cT_in = self.inp("cT", [128, NFC, 2])
        self.w_mod = self.inp("w_mod", [dp, D, 6 * D])
        self.b_modT = self.inp("b_modT", [dp, 128, 96])
        self.nmixT = self.inp("nmixT", [dp, 128, NFC])
        self.nffnT = self.inp("nffnT", [dp, 128, NFC])
        self.w_in = self.inp("w_in", [dp, D, IN_W])
        self.qkn = self.inp("qkn", [dp, 128, 2])
        self.w_attn_o = self.inp("w_attn_o", [dp, 1024, D])
        self.ccw = self.inp("ccw", [dp, 128, 8, 31])
        self.ccb = self.inp("ccb", [dp, 128, 8, 3])
        self.w_conf_o = self.inp("w_conf_o", [dp, 1024, D])
        self.scw = self.inp("scw", [dp, 128, 12, 6])
        self.srow = self.inp("srow", [dp, 128, 4, 32])
        self.snw = self.inp("snw", [dp, 128, 8])
        self.w_ssm_o = self.inp("w_ssm_o", [dp, 1024, D])
        self.w_out = self.inp("w_out", [dp, D, D])
        self.w_up = self.inp("w_up", [dp, D, 2 * DFF])
        self.fcw = self.inp("fcw", [dp, 128, 88, 4])
        self.w_down = self.inp("w_down", [dp, DFF, D])
        self.fnwT = self.inp("fnwT", [128, NFC])
        self.cosT = self.inp("cosT", [128, T])
        self.sinT = self.inp("sinT", [128, T])
        self.consts_in = self.inp("consts", [128, 7, 128])
        self.out = self.fw.dram("out", [L, D], F32, kind="ExternalOutput")

        S = self.scratch
        self.xT = S("xT", [D, T], F32)
        self.qT = S("qT", [1024, T], BF16)
        self.kT = S("kT", [256, T], BF16)
        self.v_tm = S("v_tm", [T, 256], BF16)
        self.attT = S("attT", [1024, T], BF16)
        self.cfa = S("cfa", [1024, T], BF16)
        self.cfg = S("cfg", [1024, T], BF16)
        self.confT = S("confT", [1024, T], BF16)
        self.cfv = S("cfv", [1024, T], BF16)
        self.xcv = S("xcv", [1536, T], BF16)
        self.z_tm = S("z_tm", [T, 1024], BF16)
        self.xbc = S("xbc", [1536, T], BF16)
        self.dtd = S("dtd", [T, 32], F32)
        self.gat = S("gat", [6144, T], BF16)
        self.ytmp = S("ytmp", [T, 1024], F32)
        self.ytf = S("ytf", [T, 1024], F32)
        self.ssyT = S("ssyT", [1024, T], BF16)
        self.mrgT = S("mrgT", [D, T], BF16)
        self.ffaT = S("ffaT", [DFF, T], BF16)

        with ExitStack() as top:
            fw.stack = top
            self.cf = fw.sb("cf", [128, 7, 128], F32)
            fw.dma(fw.sp, self.cf[:], self.consts_in.t.ap(), self.cf, reads=[self.consts_in], writes=[self.cf])
            self.cb = fw.sb("cb", [128, 7, 128], BF16)
            fw.op(fw.dve, lambda e: e.tensor_copy(self.cb[:], self.cf[:]), reads=[self.cf], writes=[self.cb])
            self.identf = self.cf[:, 0, :]
            self.ident = self.cb[:, 0, :]
            self.ones = self.cb[:, 1, :]
            self.onesf = self.cf[:, 1, :]
            self.Rm = self.cb[:, 2, :]
            self.epsc = fw.sb("epsc", [128, 1], F32)
            fw.op(fw.pool, lambda e: e.memset(self.epsc[:], EPS), writes=[self.epsc])
            self.mTs = [fw.sb("mT%d" % i, [128, 96, 2], F32) for i in range(2)]
            self.g1s = [fw.sb("g1%d" % i, [128, NFC, 2], F32) for i in range(2)]
            self.g2s = [fw.sb("g2%d" % i, [128, NFC, 2], F32) for i in range(2)]
            self.CONST = [self.cf, self.cb, self.epsc]
            fw.barrier()

            self.init_phase()
            for l in range(dp):
                self.mT, self.g1, self.g2 = self.mTs[l % 2], self.g1s[l % 2], self.g2s[l % 2]
                self.norm_and_proj(l, 0)
                with self.phase():
                    ad = self.adaln_steps(l + 1) if l + 1 < dp else []
                    h_ = (len(ad) * 3) // 5
                    self.attention(l, ad[:h_])
                    self.conformer(l)
                    self.ssd(l, ad[h_:])
                self.oproj(l)
                self.wout(l)
                self.norm_and_proj(l, 1)
                self.ffn_down(l)
            self.final_phase()
            fw.barrier()
        return nc

    def xT_tile(self, t0, n):
        return self.xT.t.ap().rearrange("(c p) t -> p c t", p=128)[:, :, t0:t0 + n]

    def fm(self, buf, nchunk, t0, n, c0=0):
        return buf.t.ap().rearrange("(c p) t -> p c t", p=128)[:, c0:c0 + nchunk, t0:t0 + n]

    def init_phase(self):
        fw = self.fw
        with self.phase():
            xin = [fw.sb("i_x%d" % i, [128, D], F32) for i in range(2)]
            st = [fw.sb("i_s%d" % i, [128, NFC, 128], F32) for i in range(2)]
            pt = [fw.ps("i_p%d" % i, [128, 512], F32) for i in range(4)]
            side = self.adaln_steps(0)
            for tc in range(self.NCH):
                if side:
                    side.pop(0)()
                xi, so = xin[tc % 2], st[tc % 2]
                src = self.ctx_in if tc < 2 else self.x_in
                r0 = tc * 128 if tc < 2 else (tc - 2) * 128
                fw.dma(fw.sp, xi[:], src.t.ap()[r0:r0 + 128, :], xi, reads=[src], writes=[xi])
                for q in range(4):
                    p = pt[q]
                    for j in range(4):
                        fc = q * 4 + j
                        fw.op(fw.pe, lambda e: e.transpose(p[:, j * 128:(j + 1) * 128], xi[:, fc * 128:(fc + 1) * 128], self.identf),
                              reads=[xi], writes=[p])
                    eng = fw.dve if q % 2 == 0 else fw.act
                    if q % 2 == 0:
                        fw.op(fw.dve, lambda e: e.tensor_copy(so[:, q * 4:(q + 1) * 4, :], p[:].rearrange("p (a b) -> p a b", a=4)),
                              reads=[p], writes=[so])
                    else:
                        fw.op(fw.act, lambda e: e.copy(so[:, q * 4:(q + 1) * 4, :], p[:].rearrange("p (a b) -> p a b", a=4)),
                              reads=[p], writes=[so])
                fw.dma(fw.sp, self.xT_tile(tc * 128, 128), so[:], so, reads=[so], writes=[self.xT])
            while side:
                side.pop(0)()

    def adaln_steps(self, l):
        fw = self.fw
        mT, g1, g2 = self.mTs[l % 2], self.g1s[l % 2], self.g2s[l % 2]
        cT = fw.sb("a_c", [128, NFC, 2], F32)
        sc = fw.sb("a_sc", [128, NFC, 2], BF16)
        bm = fw.sb("a_bm", [128, 96], F32)
        nw = fw.sb("a_nw", [128, 2, NFC], F32)
        ws = [fw.sb("a_w%d" % i, [128, NFC, 512], BF16) for i in range(2)]
        pm = fw.ps("a_pm", [128, 96, 2], F32)
        wsrc = self.w_mod.t.ap()[l].rearrange("(kc p) n -> p kc n", p=128)

        def load(g):
            fw.dma(fw.pool, ws[g % 2][:], wsrc[:, :, g * 512:(g + 1) * 512], ws[g % 2], reads=[self.w_mod], writes=[ws[g % 2]])

        def setup():
            fw.dma(fw.sp, cT[:], self.cT_in.t.ap(), cT, reads=[self.cT_in], writes=[cT])
            fw.dma(fw.sp, bm[:], self.b_modT.t.ap()[l], bm, reads=[self.b_modT], writes=[bm])
            fw.dma(fw.sp, nw[:, 0, :], self.nmixT.t.ap()[l], nw, reads=[self.nmixT], writes=[nw])
            fw.dma(fw.sp, nw[:, 1, :], self.nffnT.t.ap()[l], nw, reads=[self.nffnT], writes=[nw])
            fw.op(fw.act, lambda e: e.activation(sc[:], cT[:], AF.Silu), reads=[cT], writes=[sc])
            load(0)

        def step(g):
            def f():
                w = ws[g % 2]
                if g + 1 < 24:
                    load(g + 1)
                for j in range(4):
                    oc = g * 4 + j
                    for kc in range(NFC):
                        fw.op(fw.pe, lambda e: e.matmul(pm[:, oc, :], w[:, kc, j * 128:(j + 1) * 128], sc[:, kc, :],
                                                        start=(kc == 0), stop=(kc == NFC - 1)),
                              reads=[w, sc], writes=[pm])
            return f

        def finish():
            for c in range(2):
                fw.op(fw.dve, lambda e: e.tensor_tensor(mT[:, :, c], pm[:, :, c], bm[:], ALU.add), reads=[pm, bm], writes=[mT])
            for c in range(2):
                fw.op(fw.dve, lambda e: e.scalar_tensor_tensor(g1[:, :, c], mT[:, 16:32, c], 1.0, nw[:, 0, :], ALU.add, ALU.mult),
                      reads=[mT, nw], writes=[g1])
                fw.op(fw.dve, lambda e: e.scalar_tensor_tensor(g2[:, :, c], mT[:, 64:80, c], 1.0, nw[:, 1, :], ALU.add, ALU.mult),
                      reads=[mT, nw], writes=[g2])

        return [setup] + [step(g) for g in range(24)] + [finish]

    def conf_conv_steps(self, l):
        fw, T, L = self.fw, self.T, self.L
        W = L + 301
        c_ctx, c_lat = 15, 286
        NO = L + 271
        a = fw.sb("cf_a", [128, T], BF16)
        sg = fw.sb("cf_g", [128, T], BF16)
        U = fw.sb("cf_U", [128, W], BF16)
        acc = fw.sb("cf_acc", [128, W], F32)
        vo = fw.sb("cf_vo", [128, T], BF16)
        cw = fw.sb("cf_w", [128, 8, 31], F32)
        cbb = fw.sb("cf_b", [128, 8, 3], F32)
        steps = []

        def setup():
            fw.dma(fw.sp, cw[:], self.ccw.t.ap()[l], cw, reads=[self.ccw], writes=[cw])
            fw.dma(fw.sp, cbb[:], self.ccb.t.ap()[l], cbb, reads=[self.ccb], writes=[cbb])
            fw.op(fw.pool, lambda e: e.memset(U[:], 0.0), writes=[U])
        steps.append(setup)
        for cc in range(8):
            def s0(cc=cc):
                fw.dma(fw.sp, a[:], self.cfa.t.ap()[cc * 128:(cc + 1) * 128, :], a, reads=[self.cfa], writes=[a])
                fw.dma(fw.sp, sg[:], self.cfg.t.ap()[cc * 128:(cc + 1) * 128, :], sg, reads=[self.cfg], writes=[sg])
                fw.op(fw.pool, lambda e: e.tensor_tensor(U[:, c_ctx:c_ctx + CTX], a[:, 0:CTX], sg[:, 0:CTX], ALU.mult), reads=[a, sg], writes=[U])
                fw.op(fw.pool, lambda e: e.tensor_tensor(U[:, c_lat:c_lat + L], a[:, CTX:T], sg[:, CTX:T], ALU.mult), reads=[a, sg], writes=[U])
                fw.op(fw.dve, lambda e: e.tensor_scalar(acc[:, 15:15 + NO], U[:, 0:NO], cw[:, cc, 0:1], cbb[:, cc, 0:1], ALU.mult, ALU.add),
                      reads=[U, cw, cbb], writes=[acc])
            steps.append(s0)
            for k0 in range(1, 31, 3):
                def st(cc=cc, k0=k0):
                    for k in range(k0, min(k0 + 3, 31)):
                        fw.op(fw.dve, lambda e: e.scalar_tensor_tensor(acc[:, 15:15 + NO], U[:, k:k + NO], cw[:, cc, k:k + 1], acc[:, 15:15 + NO], ALU.mult, ALU.add),
                              reads=[U, cw, acc], writes=[acc])
                steps.append(st)

            def s9(cc=cc):
                fw.op(fw.pool, lambda e: e.tensor_copy(vo[:, 0:CTX], acc[:, c_ctx:c_ctx + CTX]), reads=[acc], writes=[vo])
                fw.op(fw.pool, lambda e: e.tensor_copy(vo[:, CTX:T], acc[:, c_lat:c_lat + L]), reads=[acc], writes=[vo])
                fw.dma(fw.pool, self.cfv.t.ap()[cc * 128:(cc + 1) * 128, :], vo[:], vo, reads=[vo], writes=[self.cfv])
            steps.append(s9)
        return steps

    def rms_tile(self, xi, n, sq, pss, rt, rstd, width):
        fw = self.fw
        fw.op(fw.act, lambda e: e.activation(sq[:, :, :n], xi[:, :, :n], AF.Square), reads=[xi], writes=[sq])
        for fc in range(NFC):
            fw.op(fw.pe, lambda e: e.matmul(pss[:, :n], self.ones, sq[:, fc, :n], start=(fc == 0), stop=(fc == NFC - 1)),
                  reads=[sq], writes=[pss])
        fw.op(fw.act, lambda e: e.activation(rt[:, :n], pss[:, :n], AF.Sqrt, bias=self.epsc[:, 0:1], scale=1.0 / width),
              reads=[pss], writes=[rt])
        fw.op(fw.dve, lambda e: e.reciprocal(rstd[:, :n], rt[:, :n]), reads=[rt], writes=[rstd])

    def norm_and_proj(self, l, which):
        fw, T = self.fw, self.T
        with self.phase():
            hT = fw.sb("hT", [128, NFC, T], BF16)
            gsc = self.g1 if which == 0 else self.g2
            shift_off = 0 if which == 0 else 48
            with self.phase():
                xis = [fw.sb("n_x%d" % i, [128, NFC, 256], F32) for i in range(2)]
                sq = fw.sb("n_sq", [128, NFC, 256], BF16)
                rt = fw.sb("n_rt", [128, 256], F32)
                rstd = fw.sb("n_rs", [128, 256], F32)
                tmp = [fw.sb("n_t%d" % i, [128, 256], F32) for i in range(2)]
                pss = fw.ps("n_ps", [128, 512], F32)
                subs = [(t0 + o_, 256, c) for (t0, n, c) in self.tiles for o_ in range(0, n, 256)]

                def load(i):
                    t0, n, c = subs[i]
                    fw.dma(fw.sp, xis[i % 2][:, :, :n], self.xT_tile(t0, n), xis[i % 2], reads=[self.xT], writes=[xis[i % 2]])

                load(0)
                for i, (t0, n, c) in enumerate(subs):
                    xi = xis[i % 2]
                    if i + 1 < len(subs):
                        load(i + 1)
                    self.rms_tile(xi, n, sq, pss, rt, rstd, D)
                    for fc in range(NFC):
                        tm = tmp[fc % 2]
                        fw.op(fw.dve, lambda e: e.scalar_tensor_tensor(tm[:, :n], xi[:, fc, :n], gsc[:, fc, c:c + 1], rstd[:, :n], ALU.mult, ALU.mult),
                              reads=[xi, rstd, gsc], writes=[tm])
                        fw.op(fw.act, lambda e: e.activation(hT[:, fc, t0:t0 + n], tm[:, :n], AF.Identity,
                                                             bias=self.mT[:, shift_off + fc, c:c + 1], scale=1.0),
                              reads=[tm, self.mT], writes=[hT])
            if which == 0:
                self.win(l, hT)
            else:
                self.ffn_up(l, hT)

    def win(self, l, hT):
        fw, T, L = self.fw, self.T, self.L
        with self.phase():
            ws = [fw.sb("w_w%d" % i, [128, NFC, 256], BF16) for i in range(2)]
            ob = [fw.sb("w_o%d" % i, [128, T], BF16) for i in range(2)]
            pm = [fw.ps("w_p%d" % i, [128, 512], F32) for i in range(4)]
            pss = fw.ps("w_pss", [128, 512], F32)
            prot = fw.ps("w_prot", [128, 512], F32)
            ptm = fw.ps("w_ptm", [128, 512], F32)
            sq = fw.sb("w_sq", [128, 512], BF16)
            rt = fw.sb("w_rt", [128, 512], F32)
            rstd = fw.sb("w_rs", [128, 512], F32)
            qn = fw.sb("w_qn", [128, 512], BF16)
            css = [fw.sb("w_cs%d" % i, [128, 2, 512], F32) for i in range(3)]
            t1 = fw.sb("w_t1", [128, 512], F32)
            t2 = fw.sb("w_t2", [128, 512], F32)
            qk = fw.sb("w_qk", [128, 2], F32)
            vst = [fw.sb("w_vs%d" % i, [128, 512], BF16) for i in range(2)]
            dst = fw.sb("w_ds", [128, 32], F32)
            dst2 = fw.sb("w_ds2", [128, 32], F32)
            srow = fw.sb("w_srow", [128, 4, 32], F32)
            wdt = fw.sb("w_wdt", [128, NFC, 32], BF16)
            fw.dma(fw.sp, qk[:], self.qkn.t.ap()[l], qk, reads=[self.qkn], writes=[qk])
            fw.dma(fw.sp, srow[:], self.srow.t.ap()[l], srow, reads=[self.srow], writes=[srow])
            wsrc = self.w_in.t.ap()[l].rearrange("(kc p) n -> p kc n", p=128)
            cnt = {"w": 0, "o": 0, "p": 0, "v": 0}

            def load_w(c0, ncol):
                w = ws[cnt["w"] % 2]
                cnt["w"] += 1
                fw.dma(fw.pool, w[:, :, :ncol], wsrc[:, :, c0:c0 + ncol], w, reads=[self.w_in], writes=[w])
                return w

            def fm_chunk(w, j, post, dest, dchunk):
                o = ob[cnt["o"] % 2]
                cnt["o"] += 1
                for (t0, n, c) in self.tiles:
                    p = pm[cnt["p"] % 4]
                    cnt["p"] += 1
                    for kc in range(NFC):
                        fw.op(fw.pe, lambda e: e.matmul(p[:, :n], w[:, kc, j * 128:(j + 1) * 128], hT[:, kc, t0:t0 + n],
                                                        start=(kc == 0), stop=(kc == NFC - 1)),
                              reads=[w, hT], writes=[p])
                    post(p, o, t0, n)
                fw.dma(fw.sp, dest.t.ap()[dchunk * 128:(dchunk + 1) * 128, :], o[:], o, reads=[o], writes=[dest])

            def post_act(func):
                def f(p, o, t0, n):
                    fw.op(fw.act, lambda e: e.activation(o[:, t0:t0 + n], p[:, :n], func), reads=[p], writes=[o])
                return f

            def fm_chunk_qk(w, j, col, dest, dchunk):
                o = ob[cnt["o"] % 2]
                cnt["o"] += 1
                tl = self.tiles
                ps_of = {}

                def A(i):
                    t0, n, c = tl[i]
                    p = ps_of[i]
                    csb = css[i % 3]
                    fw.dma(fw.sp, csb[:, 0, :n], self.cosT.t.ap()[:, t0:t0 + n], csb, reads=[self.cosT], writes=[csb])
                    fw.dma(fw.sp, csb[:, 1, :n], self.sinT.t.ap()[:, t0:t0 + n], csb, reads=[self.sinT], writes=[csb])
                    fw.op(fw.act, lambda e: e.activation(sq[:, :n], p[:, :n], AF.Square), reads=[p], writes=[sq])

                def B(i):
                    t0, n, c = tl[i]
                    p = ps_of[i]
                    fw.op(fw.pe, lambda e: e.matmul(pss[:, :n], self.ones, sq[:, :n], start=True, stop=True), reads=[sq], writes=[pss])
                    fw.op(fw.act, lambda e: e.activation(rt[:, :n], pss[:, :n], AF.Sqrt, bias=self.epsc[:, 0:1], scale=1.0 / 128),
                          reads=[pss], writes=[rt])
                    fw.op(fw.dve, lambda e: e.reciprocal(rstd[:, :n], rt[:, :n]), reads=[rt], writes=[rstd])
                    fw.op(fw.dve, lambda e: e.scalar_tensor_tensor(qn[:, :n], p[:, :n], qk[:, col:col + 1], rstd[:, :n], ALU.mult, ALU.mult),
                          reads=[p, rstd, qk], writes=[qn])

                def C(i):
                    t0, n, c = tl[i]
                    csb = css[i % 3]
                    fw.op(fw.pe, lambda e: e.matmul(prot[:, :n], self.Rm, qn[:, :n], start=True, stop=True), reads=[qn], writes=[prot])
                    fw.op(fw.pool, lambda e: e.tensor_tensor(t1[:, :n], qn[:, :n], csb[:, 0, :n], ALU.mult), reads=[qn, csb], writes=[t1])
                    fw.op(fw.dve, lambda e: e.tensor_tensor(t2[:, :n], prot[:, :n], csb[:, 1, :n], ALU.mult), reads=[prot, csb], writes=[t2])
                    fw.op(fw.pool, lambda e: e.tensor_tensor(o[:, t0:t0 + n], t1[:, :n], t2[:, :n], ALU.add), reads=[t1, t2], writes=[o])

                nt = len(tl)
                for i in range(nt + 2):
                    if i < nt:
                        t0, n, c = tl[i]
                        p = pm[cnt["p"] % 4]
                        cnt["p"] += 1
                        ps_of[i] = p
                        for kc in range(NFC):
                            fw.op(fw.pe, lambda e: e.matmul(p[:, :n], w[:, kc, j * 128:(j + 1) * 128], hT[:, kc, t0:t0 + n],
                                                            start=(kc == 0), stop=(kc == NFC - 1)),
                                  reads=[w, hT], writes=[p])
                    if 0 <= i - 2 < nt:
                        C(i - 2)
                    if 0 <= i - 1 < nt:
                        B(i - 1)
                    if i < nt:
                        A(i)
                fw.dma(fw.sp, dest.t.ap()[dchunk * 128:(dchunk + 1) * 128, :], o[:], o, reads=[o], writes=[dest])

            def tm_chunk(w, c_lo, ncol, dest, dcol0, func):
                for tc in range(self.NCH):
                    for kc in range(NFC):
                        fw.op(fw.pe, lambda e: e.matmul(ptm[:, :ncol], hT[:, kc, tc * 128:(tc + 1) * 128], w[:, kc, c_lo:c_lo + ncol],
                                                        start=(kc == 0), stop=(kc == NFC - 1)),
                              reads=[w, hT], writes=[ptm])
                    vs = vst[cnt["v"] % 2]
                    cnt["v"] += 1
                    fw.op(fw.act, lambda e: e.activation(vs[:, :ncol], ptm[:, :ncol], func), reads=[ptm], writes=[vs])
                    fw.dma(fw.sp, dest.t.ap()[tc * 128:(tc + 1) * 128, dcol0:dcol0 + ncol], vs[:, :ncol], vs, reads=[vs], writes=[dest])

            jobs = []

            def J(c0, fn):
                jobs.append((c0, fn))

            for s_ in range(5):
                def f(w, s_=s_):
                    for j in range(2):
                        oc = s_ * 2 + j
                        if oc < 8:
                            fm_chunk_qk(w, j, 0, self.qT, oc)
                        else:
                            fm_chunk_qk(w, j, 1, self.kT, oc - 8)
                J(s_ * 256, f)
            J(1280, lambda w: tm_chunk(w, 0, 256, self.v_tm, 0, AF.Copy))
            for s_ in range(4):
                def f(w, s_=s_):
                    for j in range(2):
                        fm_chunk(w, j, post_act(AF.Copy), self.cfa, s_ * 2 + j)
                J(1536 + s_ * 256, f)
            for s_ in range(4):
                def f(w, s_=s_):
                    for j in range(2):
                        fm_chunk(w, j, post_act(AF.Sigmoid), self.cfg, s_ * 2 + j)
                J(2560 + s_ * 256, f)
            for s_ in range(4):
                J(3584 + s_ * 256, lambda w, s_=s_: tm_chunk(w, 0, 256, self.z_tm, s_ * 256, AF.Silu))
            for s_ in range(6):
                def f(w, s_=s_):
                    for j in range(2):
                        fm_chunk(w, j, post_act(AF.Copy), self.xbc, s_ * 2 + j)
                J(4608 + s_ * 256, f)
            for s_ in range(24):
                def f(w, s_=s_):
                    for j in range(2):
                        fm_chunk(w, j, post_act(AF.Sigmoid), self.gat, s_ * 2 + j)
                J(6176 + s_ * 256, f)

            fw.dma(fw.pool, wdt[:], wsrc[:, :, 6144:6176], wdt, reads=[self.w_in], writes=[wdt])
            wcur = load_w(jobs[0][0], 256)
            for tc in range(self.NCH):
                for kc in range(NFC):
                    fw.op(fw.pe, lambda e: e.matmul(ptm[:, :32], hT[:, kc, tc * 128:(tc + 1) * 128], wdt[:, kc, :],
                                                    start=(kc == 0), stop=(kc == NFC - 1)),
                          reads=[wdt, hT], writes=[ptm])
                fw.op(fw.dve, lambda e: e.tensor_tensor(dst[:], ptm[:, :32], srow[:, 1, :], ALU.add), reads=[ptm, srow], writes=[dst])
                fw.op(fw.act, lambda e: e.activation(dst[:], dst[:], AF.Exp), reads=[dst], writes=[dst])
                fw.op(fw.dve, lambda e: e.tensor_scalar(dst2[:], dst[:], 1.0, None, ALU.add), reads=[dst], writes=[dst2])
                fw.op(fw.act, lambda e: e.activation(dst2[:], dst2[:], AF.Ln), reads=[dst2], writes=[dst2])
                fw.dma(fw.sp, self.dtd.t.ap()[tc * 128:(tc + 1) * 128, :], dst2[:], dst2, reads=[dst2], writes=[self.dtd])
            for i, (c0, fn) in enumerate(jobs):
                wnext = load_w(jobs[i + 1][0], 256) if i + 1 < len(jobs) else None
                fn(wcur)
                wcur = wnext

    def attention(self, l, ad=()):
        fw, T = self.fw, self.T
        NCH = self.NCH
        with self.phase():
            kT = fw.sb("at_k", [128, 2, T], BF16)
            vt = fw.sb("at_v", [128, NCH, 256], BF16)
            qt = [fw.sb("at_q%d" % i, [128, 512], BF16) for i in range(2)]
            pT = [fw.sb("at_p%d" % i, [128, 512], BF16) for i in range(3)]
            rs = fw.sb("at_rs", [128, 512], F32)
            ob = [fw.sb("at_o%d" % i, [128, 512], BF16) for i in range(2)]
            ps_s = [fw.ps("at_s%d" % i, [128, 512], F32) for i in range(3)]
            ps_o = [fw.ps("at_a%d" % i, [128, 512], F32) for i in range(2)]
            ps_r = [fw.ps("at_r%d" % i, [128, 512], F32) for i in range(2)]
            fw.dma(fw.sp, kT[:], self.fm(self.kT, 2, 0, T), kT, reads=[self.kT], writes=[kT])
            fw.dma(fw.sp, vt[:], self.v_tm.t.ap().rearrange("(c p) d -> p c d", p=128), vt, reads=[self.v_tm], writes=[vt])
            side = self.conf_conv_steps(l)
            sc_ = self.ssd_conv_steps(l)
            k_ = max(1, len(side) // len(sc_))
            mix = []
            while side or sc_:
                for _ in range(k_):
                    if side:
                        mix.append(side.pop(0))
                if sc_:
                    mix.append(sc_.pop(0))
            side = mix
            ad = list(ad)
            if ad:
                merged = []
                for i in range(max(len(side), len(ad))):
                    if i < len(side):
                        merged.append(side[i])
                    if i < len(ad):
                        merged.append(ad[i])
                side = merged
            iters = [(h, t0, n, c) for h in range(8) for (t0, n, c) in self.tiles]

            def load_q(i):
                h, t0, n, c = iters[i]
                q = qt[i % 2]
                fw.dma(fw.sp, q[:, :n], self.qT.t.ap()[h * 128:(h + 1) * 128, t0:t0 + n], q, reads=[self.qT], writes=[q])

            load_q(0)
            ip = 0
            for it, (h, t0, n, c) in enumerate(iters):
                g = h // 4
                nk = 2 if c == 1 else NCH
                q = qt[it % 2]
                o = ob[it % 2]
                pso, psr = ps_o[it % 2], ps_r[it % 2]
                if it + 1 < len(iters):
                    load_q(it + 1)
                nside = (len(side) + len(iters) - 1) // len(iters)
                for _ in range(nside):
                    if side:
                        side.pop(0)()

                def S(kc):
                    pss = ps_s[(ip + kc) % 3]
                    fw.op(fw.pe, lambda e: e.matmul(pss[:, :n], kT[:, g, kc * 128:(kc + 1) * 128], q[:, :n], start=True, stop=True),
                          reads=[kT, q], writes=[pss])

                S(0)
                if nk > 1:
                    S(1)
                for kc in range(nk):
                    pss = ps_s[(ip + kc) % 3]
                    p = pT[(ip + kc) % 3]
                    fw.op(fw.act, lambda e: e.activation(p[:, :n], pss[:, :n], AF.Exp, scale=ATT_SCALE), reads=[pss], writes=[p])
                    if kc + 2 < nk:
                        S(kc + 2)
                    fw.op(fw.pe, lambda e: e.matmul(pso[:, :n], vt[:, kc, g * 128:(g + 1) * 128], p[:, :n], start=(kc == 0), stop=(kc == nk - 1)),
                          reads=[vt, p], writes=[pso])
                    fw.op(fw.pe, lambda e: e.matmul(psr[:, :n], self.ones, p[:, :n], start=(kc == 0), stop=(kc == nk - 1)),
                          reads=[p], writes=[psr])
                ip += nk
                fw.op(fw.dve, lambda e: e.reciprocal(rs[:, :n], psr[:, :n]), reads=[psr], writes=[rs])
                fw.op(fw.dve, lambda e: e.tensor_tensor(o[:, :n], pso[:, :n], rs[:, :n], ALU.mult), reads=[pso, rs], writes=[o])
                fw.dma(fw.sp, self.attT.t.ap()[h * 128:(h + 1) * 128, t0:t0 + n], o[:, :n], o, reads=[o], writes=[self.attT])
            while side:
                side.pop(0)()

    def conformer(self, l):
        fw, T, L = self.fw, self.T, self.L
        with self.phase():
            cbb = fw.sb("cf_b", [128, 8, 3], F32)
            fw.dma(fw.sp, cbb[:], self.ccb.t.ap()[l], cbb, reads=[self.ccb], writes=[cbb])
            vin = [fw.sb("cf_vi%d" % i, [128, 8, 512], BF16) for i in range(2)]
            vsq = fw.sb("cf_sq", [128, 8, 512], BF16)
            mean = fw.sb("cf_m", [128, 512], F32)
            msq = fw.sb("cf_m2", [128, 512], F32)
            var = fw.sb("cf_var", [128, 512], F32)
            rt = fw.sb("cf_rt", [128, 512], F32)
            rstd = fw.sb("cf_rs", [128, 512], F32)
            tt = [fw.sb("cf_t%d" % i, [128, 512], F32) for i in range(2)]
            ob = [fw.sb("cf_o%d" % i, [128, 8, 512], BF16) for i in range(2)]
            p1 = fw.ps("cf_p1", [128, 512], F32)
            p2 = fw.ps("cf_p2", [128, 512], F32)
            def load_v(ti):
                t0, n, c = self.tiles[ti]
                fw.dma(fw.sp, vin[ti % 2][:, :, :n], self.fm(self.cfv, 8, t0, n), vin[ti % 2], reads=[self.cfv], writes=[vin[ti % 2]])

            load_v(0)
            for ti, (t0, n, c) in enumerate(self.tiles):
                o = ob[ti % 2]
                vres = vin[ti % 2]
                if ti + 1 < len(self.tiles):
                    load_v(ti + 1)
                fw.op(fw.act, lambda e: e.activation(vsq[:, :, :n], vres[:, :, :n], AF.Square), reads=[vres], writes=[vsq])
                for cc in range(8):
                    fw.op(fw.pe, lambda e: e.matmul(p1[:, :n], self.ones, vres[:, cc, :n], start=(cc == 0), stop=(cc == 7)),
                          reads=[vres], writes=[p1])
                for cc in range(8):
                    fw.op(fw.pe, lambda e: e.matmul(p2[:, :n], self.ones, vsq[:, cc, :n], start=(cc == 0), stop=(cc == 7)),
                          reads=[vsq], writes=[p2])
                fw.op(fw.dve, lambda e: e.tensor_scalar(mean[:, :n], p1[:, :n], 1.0 / 1024, None, ALU.mult), reads=[p1], writes=[mean])
                fw.op(fw.dve, lambda e: e.tensor_tensor(msq[:, :n], mean[:, :n], mean[:, :n], ALU.mult), reads=[mean], writes=[msq])
                fw.op(fw.dve, lambda e: e.scalar_tensor_tensor(var[:, :n], p2[:, :n], 1.0 / 1024, msq[:, :n], ALU.mult, ALU.subtract),
                      reads=[p2, msq], writes=[var])
                fw.op(fw.act, lambda e: e.activation(rt[:, :n], var[:, :n], AF.Sqrt, bias=self.epsc[:, 0:1], scale=1.0), reads=[var], writes=[rt])
                fw.op(fw.dve, lambda e: e.reciprocal(rstd[:, :n], rt[:, :n]), reads=[rt], writes=[rstd])
                for cc in range(8):
                    t = tt[cc % 2]
                    fw.op(fw.dve, lambda e: e.tensor_tensor(t[:, :n], vres[:, cc, :n], mean[:, :n], ALU.subtract), reads=[vres, mean], writes=[t])
                    fw.op(fw.pool if cc % 2 else fw.dve, lambda e: e.tensor_tensor(t[:, :n], t[:, :n], rstd[:, :n], ALU.mult), reads=[t, rstd], writes=[t])
                    fw.op(fw.act, lambda e: e.activation(o[:, cc, :n], t[:, :n], AF.Silu, bias=cbb[:, cc, 2:3], scale=cbb[:, cc, 1:2]),
                          reads=[t, cbb], writes=[o])
                fw.dma(fw.sp, self.fm(self.confT, 8, t0, n), o[:, :, :n], o, reads=[o], writes=[self.confT])

    def ssd_conv_steps(self, l):
        fw, T, L = self.fw, self.T, self.L
        W = L + 262
        c_ctx, c_lat = 2, 260
        NO = L + 258
        xr = fw.sb("s_xr", [128, T], BF16)
        U = fw.sb("s_U", [128, W], BF16)
        acc = fw.sb("s_acc", [128, W], F32)
        xo = fw.sb("s_xo", [128, T], BF16)
        cw = fw.sb("s_cw", [128, 12, 6], F32)
        steps = []

        def setup():
            fw.dma(fw.sp, cw[:], self.scw.t.ap()[l], cw, reads=[self.scw], writes=[cw])
            fw.op(fw.pool, lambda e: e.memset(U[:], 0.0), writes=[U])
        steps.append(setup)
        for cc in range(12):
            def st(cc=cc):
                fw.dma(fw.sp, xr[:], self.xbc.t.ap()[cc * 128:(cc + 1) * 128, :], xr, reads=[self.xbc], writes=[xr])
                fw.op(fw.pool, lambda e: e.tensor_copy(U[:, c_ctx:c_ctx + CTX], xr[:, 0:CTX]), reads=[xr], writes=[U])
                fw.op(fw.pool, lambda e: e.tensor_copy(U[:, c_lat:c_lat + L], xr[:, CTX:T]), reads=[xr], writes=[U])
                fw.op(fw.dve, lambda e: e.tensor_scalar(acc[:, 2:2 + NO], U[:, 0:NO], cw[:, cc, 0:1], cw[:, cc, 5:6], ALU.mult, ALU.add),
                      reads=[U, cw], writes=[acc])
                for k in range(1, 5):
                    fw.op(fw.dve, lambda e: e.scalar_tensor_tensor(acc[:, 2:2 + NO], U[:, k:k + NO], cw[:, cc, k:k + 1], acc[:, 2:2 + NO], ALU.mult, ALU.add),
                          reads=[U, cw, acc], writes=[acc])
            steps.append(st)

            def st2(cc=cc):
                fw.op(fw.act, lambda e: e.activation(xo[:, 0:CTX], acc[:, c_ctx:c_ctx + CTX], AF.Silu), reads=[acc], writes=[xo])
                fw.op(fw.act, lambda e: e.activation(xo[:, CTX:T], acc[:, c_lat:c_lat + L], AF.Silu), reads=[acc], writes=[xo])
                fw.dma(fw.pool, self.xcv.t.ap()[cc * 128:(cc + 1) * 128, :], xo[:], xo, reads=[xo], writes=[self.xcv])
            steps.append(st2)
        return steps

    def ssd(self, l, ad=()):
        fw, T, L = self.fw, self.T, self.L
        NCH = self.NCH
        W = L + 262
        c_ctx, c_lat = 2, 260
        NO = L + 258
        Mf = self.cf[:, 3, :]
        Mb = self.cf[:, 4, :]
        h16 = lambda ap, h=16: ap.rearrange("p (h d) -> p h d", h=h)
        with self.phase():
            srow = fw.sb("s_srow", [128, 4, 32], F32)
            nw = fw.sb("s_nw", [128, 8], F32)
            fw.dma(fw.sp, srow[:], self.srow.t.ap()[l], srow, reads=[self.srow], writes=[srow])
            fw.dma(fw.sp, nw[:], self.snw.t.ap()[l], nw, reads=[self.snw], writes=[nw])
            dt = fw.sb("s_dt", [128, NCH, 32], F32)
            aa = fw.sb("s_a", [128, NCH, 32], F32)
            acoef = fw.sb("s_ac", [128, 32], F32)
            dsum = fw.sb("s_dsum", [128, 16], F32)
            fw.dma(fw.sp, dt[:], self.dtd.t.ap().rearrange("(c p) h -> p c h", p=128), dt, reads=[self.dtd], writes=[dt])
            fw.op(fw.act, lambda e: e.activation(acoef[:], srow[:, 0, :], AF.Exp), reads=[srow], writes=[acoef])
            fw.op(fw.dve, lambda e: e.tensor_scalar(acoef[:], acoef[:], -1.0, None, ALU.mult), reads=[acoef], writes=[acoef])
            fw.op(fw.dve, lambda e: e.tensor_tensor(aa[:], dt[:], acoef[:].unsqueeze(1).broadcast_to([128, NCH, 32]), ALU.mult),
                  reads=[dt, acoef], writes=[aa])
            fw.op(fw.dve, lambda e: e.tensor_tensor(dsum[:], srow[:, 2, 0:16], srow[:, 2, 16:32], ALU.add), reads=[srow], writes=[dsum])

            with self.phase():
                p_xs = fw.ps("s_pxs", [128, 1024], BF16)
                p_sm = fw.ps("s_psm", [128, 512], F32)
                p_bt = fw.ps("s_pbt", [128, 256], BF16)
                p_R = fw.ps("s_pR", [128, 512], F32)
                p_y = [fw.ps("s_py%d" % i, [128, 512], F32) for i in range(2)]
                p_2 = fw.ps("s_p2", [128, 512], F32)

                def make_chain(d):
                    M = Mf if d == 0 else Mb
                    NM = self.cf[:, 5 + d, :]
                    sfx = "_%d" % d
                    two = lambda nm, shp, dt_: [fw.sb(nm + sfx + "_%d" % i, shp, dt_) for i in range(2)]
                    xcb = two("s_xcb", [128, 12, 128], BF16)
                    xs_tm = two("s_xs", [128, 1024], BF16)
                    b_tm = two("s_bt", [128, 256], BF16)
                    xw = two("s_xw", [128, 1024], BF16)
                    ein = two("s_ein", [128, 16], F32)
                    edec = two("s_edec", [128, 16], F32)
                    yI = two("s_yI", [128, 1024], F32)
                    nacum = fw.sb("s_nacum" + sfx, [128, 16], F32)
                    acum = fw.sb("s_acum" + sfx, [128, 16], F32)
                    tot = fw.sb("s_tot" + sfx, [128, 16], F32)
                    eout = fw.sb("s_eout" + sfx, [128, 16], F32)
                    xdt = fw.sb("s_xdt" + sfx, [128, 1024], BF16)
                    abc = fw.sb("s_abc" + sfx, [128, 2, 16, 128], BF16)
                    ahl = fw.sb("s_ahl" + sfx, [128, 2, 16], F32)
                    ah16 = fw.sb("s_ah16" + sfx, [128, 16], BF16)
                    Mb16 = self.cb[:, 3 + d, :]
                    NMb16 = self.cb[:, 5 + d, :]
                    seg = fw.sb("s_seg" + sfx, [128, 16, 128], F32)
                    mt = fw.sb("s_mt" + sfx, [128, 16, 128], BF16)
                    cbm = fw.sb("s_cbm" + sfx, [128, 2, 128], F32)
                    H = fw.sb("s_H" + sfx, [128, 2, 512], F32)
                    Hb = fw.sb("s_Hb" + sfx, [128, 2, 512], BF16)
                    yit = fw.sb("s_yit" + sfx, [128, 1024], F32)
                    gx = fw.sb("s_gx" + sfx, [128, 1024], F32) if d == 0 else None
                    ydst = self.ytmp if d == 1 else self.ytf
                    order = list(range(NCH)) if d == 0 else [1, 0] + list(range(NCH - 1, 1, -1))
                    n = len(order)

                    def load(k):
                        tc = order[k]
                        fw.dma(fw.sp, xcb[k % 2][:], self.fm(self.xcv, 12, tc * 128, 128), xcb[k % 2], reads=[self.xcv], writes=[xcb[k % 2]])

                    def init():
                        fw.op(fw.pool, lambda e: e.memset(H[:], 0.0), writes=[H])
                        fw.op(fw.pool, lambda e: e.memset(Hb[:], 0.0), writes=[Hb])
                        load(0)

                    def stageI(k):
                        tc = order[k]
                        p = k % 2
                        xc = xcb[p]
                        a_c = aa[:, tc, d * 16:(d + 1) * 16]
                        dt_c = dt[:, tc, d * 16:(d + 1) * 16]

                        def s1():
                            if k + 1 < n:
                                load(k + 1)
                            for j in range(8):
                                fw.op(fw.pe, lambda e: e.transpose(p_xs[:, j * 128:(j + 1) * 128], xc[:, j, :], self.ident), reads=[xc], writes=[p_xs])
                            for j in range(2):
                                fw.op(fw.pe, lambda e: e.transpose(p_bt[:, j * 128:(j + 1) * 128], xc[:, 8 + j, :], self.ident), reads=[xc], writes=[p_bt])
                            fw.op(fw.act, lambda e: e.copy(xs_tm[p][:], p_xs[:]), reads=[p_xs], writes=[xs_tm[p]])
                            fw.op(fw.act, lambda e: e.copy(b_tm[p][:], p_bt[:]), reads=[p_bt], writes=[b_tm[p]])
                            fw.op(fw.pe, lambda e: e.matmul(p_sm[:, 0:16], M, a_c, start=True, stop=True), reads=[aa], writes=[p_sm])
                            fw.op(fw.pe, lambda e: e.matmul(p_sm[:, 16:32], self.onesf, a_c, start=True, stop=True), reads=[aa], writes=[p_sm])
                            fw.op(fw.dve, lambda e: e.tensor_copy(acum[:], p_sm[:, 0:16]), reads=[p_sm], writes=[acum])
                            fw.op(fw.dve, lambda e: e.tensor_copy(tot[:], p_sm[:, 16:32]), reads=[p_sm], writes=[tot])
                            fw.op(fw.dve, lambda e: e.tensor_scalar(nacum[:], p_sm[:, 0:16], -1.0, None, ALU.mult), reads=[p_sm], writes=[nacum])
                            fw.op(fw.dve, lambda e: e.tensor_copy(ah16[:], a_c), reads=[aa], writes=[ah16])
                            fw.op(fw.dve, lambda e: e.tensor_copy(ahl[:, 0, :], ah16[:]), reads=[ah16], writes=[ahl])
                            fw.op(fw.dve, lambda e: e.tensor_tensor(ahl[:, 1, :], a_c, ahl[:, 0, :], ALU.subtract), reads=[aa, ahl], writes=[ahl])
                            fw.op(fw.act, lambda e: e.activation(abc[:].rearrange("p a h i -> p (a h) i"),
                                                                 ahl[:].rearrange("p a h -> p (a h)").unsqueeze(2).broadcast_to([128, 32, 128]), AF.Copy),
                                  reads=[ahl], writes=[abc])

                        def s2():
                            fw.op(fw.act, lambda e: e.activation(ein[p][:], acum[:], AF.Exp), reads=[acum], writes=[ein[p]])
                            fw.op(fw.act, lambda e: e.activation(edec[p][:], tot[:], AF.Exp), reads=[tot], writes=[edec[p]])
                            fw.op(fw.dve, lambda e: e.tensor_tensor(eout[:], tot[:], acum[:], ALU.subtract), reads=[tot, acum], writes=[eout])
                            fw.op(fw.act, lambda e: e.activation(eout[:], eout[:], AF.Exp), reads=[eout], writes=[eout])
                            fw.op(fw.dve, lambda e: e.tensor_tensor(h16(xdt[:]), h16(xs_tm[p][:]), dt_c.unsqueeze(2).broadcast_to([128, 16, 64]), ALU.mult),
                                  reads=[xs_tm[p], dt], writes=[xdt])
                            fw.op(fw.pool, lambda e: e.tensor_tensor(h16(xw[p][:]), h16(xdt[:]), eout[:].unsqueeze(2).broadcast_to([128, 16, 64]), ALU.mult),
                                  reads=[xdt, eout], writes=[xw[p]])

                        def s3():
                            for g in range(2):
                                fw.op(fw.pe, lambda e: e.matmul(p_sm[:, 128 + g * 128:256 + g * 128], xc[:, 8 + g, :], xc[:, 10 + g, :], start=True, stop=True),
                                      reads=[xc], writes=[p_sm])
                            fw.op(fw.dve, lambda e: e.tensor_tensor(cbm[:], p_sm[:, 128:384].rearrange("p (g i) -> p g i", g=2),
                                                                    M.unsqueeze(1).broadcast_to([128, 2, 128]), ALU.mult),
                                  reads=[p_sm], writes=[cbm])
                            for hq in range(4):
                                for j in range(4):
                                    h = hq * 4 + j
                                    fw.op(fw.pe, lambda e: e.matmul(p_R[:, j * 128:(j + 1) * 128], abc[:, 0, h, :], Mb16, start=True, stop=False),
                                          reads=[abc], writes=[p_R])
                                    fw.op(fw.pe, lambda e: e.matmul(p_R[:, j * 128:(j + 1) * 128], abc[:, 1, h, :], Mb16, start=False, stop=False),
                                          reads=[abc], writes=[p_R])
                                    fw.op(fw.pe, lambda e: e.matmul(p_R[:, j * 128:(j + 1) * 128], self.ident, NMb16, start=False, stop=True),
                                          reads=[abc], writes=[p_R])
                                for j in range(4):
                                    h = hq * 4 + j
                                    fw.op(fw.act, lambda e: e.activation(seg[:, h, :], p_R[:, j * 128:(j + 1) * 128], AF.Exp, bias=nacum[:, h:h + 1], scale=1.0),
                                          reads=[p_R, nacum], writes=[seg])

                        def s4():
                            for g in range(2):
                                fw.op(fw.dve, lambda e: e.tensor_tensor(mt[:, g * 8:(g + 1) * 8, :], seg[:, g * 8:(g + 1) * 8, :],
                                                                        cbm[:, g, :].unsqueeze(1).broadcast_to([128, 8, 128]), ALU.mult),
                                      reads=[seg, cbm], writes=[mt])

                        def s5():
                            for h in range(16):
                                g = h // 8
                                fw.op(fw.pe, lambda e: e.matmul(p_y[g][:, (h % 8) * 64:(h % 8 + 1) * 64], mt[:, h, :], xdt[:, h * 64:(h + 1) * 64], start=True, stop=True),
                                      reads=[mt, xdt], writes=[p_y[g]])
                            for g in range(2):
                                fw.op(fw.act, lambda e: e.copy(yI[p][:, g * 512:(g + 1) * 512], p_y[g][:]), reads=[p_y[g]], writes=[yI[p]])
                        return [s1, s2, s3, s4, s5]

                    def stageII(k):
                        tc = order[k]
                        p = k % 2
                        xc = xcb[p]

                        def t1():
                            for g in range(2):
                                for half in range(2):
                                    fw.op(fw.pe, lambda e: e.matmul(p_2[:, 0:256], xc[:, 10 + g, :], Hb[:, g, half * 256:(half + 1) * 256], start=True, stop=True),
                                          reads=[xc, Hb], writes=[p_2])
                                    h0 = g * 8 + half * 4
                                    fw.op(fw.dve, lambda e: e.tensor_tensor(h16(yit[:, h0 * 64:(h0 + 4) * 64], 4), h16(p_2[:, 0:256], 4),
                                                                            ein[p][:, h0:h0 + 4].unsqueeze(2).broadcast_to([128, 4, 64]), ALU.mult),
                                          reads=[p_2, ein[p]], writes=[yit])

                        def t2():
                            for g in range(2):
                                fw.op(fw.pe, lambda e: e.matmul(p_2[:], b_tm[p][:, g * 128:(g + 1) * 128], xw[p][:, g * 512:(g + 1) * 512], start=True, stop=True),
                                      reads=[b_tm[p], xw[p]], writes=[p_2])
                                fw.op(fw.pool, lambda e: e.tensor_tensor(h16(H[:, g, :], 8), h16(H[:, g, :], 8),
                                                                         edec[p][:, g * 8:(g + 1) * 8].unsqueeze(2).broadcast_to([128, 8, 64]), ALU.mult),
                                      reads=[H, edec[p]], writes=[H])
                                fw.op(fw.dve, lambda e: e.tensor_tensor(H[:, g, :], H[:, g, :], p_2[:], ALU.add), reads=[H, p_2], writes=[H])
                            fw.op(fw.act, lambda e: e.copy(Hb[:], H[:]), reads=[H], writes=[Hb])

                        def t3():
                            fw.op(fw.pool, lambda e: e.tensor_tensor(yI[p][:], yI[p][:], yit[:], ALU.add), reads=[yI[p], yit], writes=[yI[p]])
                            if d == 0:
                                fw.op(fw.dve, lambda e: e.tensor_tensor(h16(gx[:]), h16(xs_tm[p][:]), dsum[:].unsqueeze(2).broadcast_to([128, 16, 64]), ALU.mult),
                                      reads=[xs_tm[p], dsum], writes=[gx])
                                fw.op(fw.pool, lambda e: e.tensor_tensor(yI[p][:], yI[p][:], gx[:], ALU.add), reads=[yI[p], gx], writes=[yI[p]])
                            fw.dma(fw.sp, ydst.t.ap()[tc * 128:(tc + 1) * 128, :], yI[p][:], yI[p], reads=[yI[p]], writes=[ydst])
                        return [t1, t2, t3]

                    seq = [init] + stageI(0)
                    for k in range(1, n):
                        a_, b_ = stageI(k), stageII(k - 1)
                        for i in range(max(len(a_), len(b_))):
                            if i < len(b_):
                                seq.append(b_[i])
                            if i < len(a_):
                                seq.append(a_[i])
                    seq += stageII(n - 1)
                    return seq

                cf_ = make_chain(0)
                cb_ = make_chain(1)
                ad = list(ad)
                nn = max(len(cf_), len(cb_))
                every = max(1, (nn - 8) // max(1, len(ad)))
                for i in range(nn):
                    if ad and i % every == 0:
                        ad.pop(0)()
                    if i < len(cf_):
                        cf_[i]()
                    if i < len(cb_):
                        cb_[i]()
                while ad:
                    ad.pop(0)()

            with self.phase():
                yf = [fw.sb("s_yf%d" % i, [128, 1024], F32) for i in range(2)]
                yb = [fw.sb("s_yb%d" % i, [128, 1024], F32) for i in range(2)]
                zt = [fw.sb("s_z%d" % i, [128, 1024], BF16) for i in range(2)]
                g1 = [fw.sb("s_g1%d" % i, [128, 1024], F32) for i in range(2)]
                sqv = fw.sb("s_sqv", [128, 1024], F32)
                g2 = [fw.sb("s_g2%d" % i, [128, 1024], BF16) for i in range(2)]
                ss = [fw.sb("s_ss%d" % i, [128, 1], F32) for i in range(2)]
                rt = [fw.sb("s_rt%d" % i, [128, 1], F32) for i in range(2)]
                rstd = [fw.sb("s_rstd%d" % i, [128, 1], F32) for i in range(2)]
                yT = [fw.sb("s_yT%d" % i, [128, 8, 128], BF16) for i in range(2)]
                p_t = [fw.ps("s_pt%d" % i, [128, 1024], BF16) for i in range(2)]

                def load(tc):
                    k = tc % 2
                    fw.dma(fw.sp, yf[k][:], self.ytf.t.ap()[tc * 128:(tc + 1) * 128, :], yf[k], reads=[self.ytf], writes=[yf[k]])
                    fw.dma(fw.sp, yb[k][:], self.ytmp.t.ap()[tc * 128:(tc + 1) * 128, :], yb[k], reads=[self.ytmp], writes=[yb[k]])
                    fw.dma(fw.sp, zt[k][:], self.z_tm.t.ap()[tc * 128:(tc + 1) * 128, :], zt[k], reads=[self.z_tm], writes=[zt[k]])

                load(0)
                for tc in range(NCH):
                    k = tc % 2
                    if tc + 1 < NCH:
                        load(tc + 1)
                    fw.op(fw.dve, lambda e: e.tensor_tensor(yf[k][:], yf[k][:], yb[k][:], ALU.add), reads=[yf[k], yb[k]], writes=[yf[k]])
                    fw.op(fw.dve, lambda e: e.tensor_tensor(g1[k][:], yf[k][:], zt[k][:], ALU.mult), reads=[yf[k], zt[k]], writes=[g1[k]])
                    fw.op(fw.act, lambda e: e.activation(sqv[:], g1[k][:], AF.Square), reads=[g1[k]], writes=[sqv])
                    fw.op(fw.dve, lambda e: e.reduce_sum(ss[k][:], sqv[:], mybir.AxisListType.X), reads=[sqv], writes=[ss[k]])
                    fw.op(fw.act, lambda e: e.activation(rt[k][:], ss[k][:], AF.Sqrt, bias=self.epsc[:, 0:1], scale=1.0 / 1024), reads=[ss[k]], writes=[rt[k]])
                    fw.op(fw.dve, lambda e: e.reciprocal(rstd[k][:], rt[k][:]), reads=[rt[k]], writes=[rstd[k]])
                    fw.op(fw.act, lambda e: e.activation(g2[k][:], g1[k][:], AF.Copy, scale=rstd[k][:, 0:1]), reads=[g1[k], rstd[k]], writes=[g2[k]])
                    for j in range(8):
                        fw.op(fw.pe, lambda e: e.transpose(p_t[k][:, j * 128:(j + 1) * 128], g2[k][:, j * 128:(j + 1) * 128], self.ident), reads=[g2[k]], writes=[p_t[k]])
                    for j in range(8):
                        fw.op(fw.dve, lambda e: e.tensor_scalar(yT[k][:, j, :], p_t[k][:, j * 128:(j + 1) * 128], nw[:, j:j + 1], None, ALU.mult),
                              reads=[p_t[k], nw], writes=[yT[k]])
                    fw.dma(fw.sp, self.fm(self.ssyT, 8, tc * 128, 128), yT[k][:], yT[k], reads=[yT[k]], writes=[self.ssyT])

    def oproj(self, l):
        fw, T = self.fw, self.T
        with self.phase():
            wo = [fw.sb("o_w%d" % i, [128, 8, D], BF16) for i in range(3)]
            srcs = [self.w_attn_o, self.w_conf_o, self.w_ssm_o]
            acts = [self.attT, self.confT, self.ssyT]
            for b in range(3):
                wsrc = srcs[b].t.ap()[l].rearrange("(kc p) n -> p kc n", p=128)
                for q in range(4):
                    fw.dma(fw.pool, wo[b][:, :, q * 512:(q + 1) * 512], wsrc[:, :, q * 512:(q + 1) * 512], wo[b], reads=[srcs[b]], writes=[wo[b]])
            xin = [[fw.sb("o_x%d_%d" % (i, b), [128, 8, 512], BF16) for b in range(3)] for i in range(2)]
            gts = [fw.sb("o_g%d" % i, [128, 3, 512], BF16) for i in range(3)]
            mo = [fw.sb("o_m%d" % i, [128, NFC, 512], BF16) for i in range(2)]
            tt = [[fw.sb("o_t%d_%d" % (i, j), [128, 512], F32) for j in range(3)] for i in range(2)]
            pp = [[fw.ps("o_p%d_%d" % (i, j), [128, 512], F32) for j in range(3)] for i in range(2)]
            iters = [(ti, oc) for ti in range(len(self.tiles)) for oc in range(NFC)]

            def load_x(ti):
                t0, n, c = self.tiles[ti]
                for b in range(3):
                    xb = xin[ti % 2][b]
                    fw.dma(fw.sp, xb[:, :, :n], self.fm(acts[b], 8, t0, n), xb, reads=[acts[b]], writes=[xb])

            def load_g(i):
                ti, oc = iters[i]
                t0, n, c = self.tiles[ti]
                gt = gts[i % 3]
                src = self.gat.t.ap().rearrange("(b c p) t -> p b c t", p=128, b=3)[:, :, oc, t0:t0 + n]
                fw.dma(fw.sp, gt[:, :, :n], src, gt, reads=[self.gat], writes=[gt])

            load_x(0)
            load_g(0)
            load_g(1)
            for i, (ti, oc) in enumerate(iters):
                t0, n, c = self.tiles[ti]
                if oc == 0 and ti + 1 < len(self.tiles):
                    load_x(ti + 1)
                if i + 2 < len(iters):
                    load_g(i + 2)
                P, Tt, gt = pp[i % 2], tt[i % 2], gts[i % 3]
                xs = xin[ti % 2]
                for b in range(3):
                    for kc in range(8):
                        fw.op(fw.pe, lambda e: e.matmul(P[b][:, :n], wo[b][:, kc, oc * 128:(oc + 1) * 128], xs[b][:, kc, :n], start=(kc == 0), stop=(kc == 7)),
                              reads=[wo[b], xs[b]], writes=[P[b]])
                    fw.op(fw.dve, lambda e: e.tensor_tensor(Tt[b][:, :n], P[b][:, :n], gt[:, b, :n], ALU.mult), reads=[P[b], gt], writes=[Tt[b]])
                fw.op(fw.pool, lambda e: e.tensor_tensor(Tt[0][:, :n], Tt[0][:, :n], Tt[1][:, :n], ALU.add), reads=[Tt[0], Tt[1]], writes=[Tt[0]])
                fw.op(fw.pool, lambda e: e.tensor_tensor(mo[ti % 2][:, oc, :n], Tt[0][:, :n], Tt[2][:, :n], ALU.add), reads=[Tt[0], Tt[2]], writes=[mo[ti % 2]])
                if oc == NFC - 1:
                    fw.dma(fw.sp, self.fm(self.mrgT, NFC, t0, n), mo[ti % 2][:, :, :n], mo[ti % 2], reads=[mo[ti % 2]], writes=[self.mrgT])

    def wout(self, l):
        fw, T = self.fw, self.T
        with self.phase():
            w = fw.sb("u_w", [128, NFC, D], BF16)
            wsrc = self.w_out.t.ap()[l].rearrange("(kc p) n -> p kc n", p=128)
            for q in range(8):
                fw.dma(fw.pool, w[:, :, q * 256:(q + 1) * 256], wsrc[:, :, q * 256:(q + 1) * 256], w, reads=[self.w_out], writes=[w])
            mi = [fw.sb("u_m%d" % i, [128, NFC, 512], BF16) for i in range(2)]
            xi = [fw.sb("u_x%d" % i, [128, NFC, 512], F32) for i in range(2)]
            pp = [fw.ps("u_p%d" % i, [128, 512], F32) for i in range(4)]

            def load(ti):
                t0, n, c = self.tiles[ti]
                fw.dma(fw.sp, mi[ti % 2][:, :, :n], self.fm(self.mrgT, NFC, t0, n), mi[ti % 2], reads=[self.mrgT], writes=[mi[ti % 2]])
                fw.dma(fw.sp, xi[ti % 2][:, :, :n], self.xT_tile(t0, n), xi[ti % 2], reads=[self.xT], writes=[xi[ti % 2]])

            load(0)
            it = 0
            for ti, (t0, n, c) in enumerate(self.tiles):
                m, x = mi[ti % 2], xi[ti % 2]
                if ti + 1 < len(self.tiles):
                    load(ti + 1)
                for oc in range(NFC):
                    p = pp[it % 4]
                    it += 1
                    for kc in range(NFC):
                        fw.op(fw.pe, lambda e: e.matmul(p[:, :n], w[:, kc, oc * 128:(oc + 1) * 128], m[:, kc, :n], start=(kc == 0), stop=(kc == NFC - 1)),
                              reads=[w, m], writes=[p])
                    fw.op(fw.dve, lambda e: e.scalar_tensor_tensor(x[:, oc, :n], p[:, :n], self.mT[:, 32 + oc, c:c + 1], x[:, oc, :n], ALU.mult, ALU.add),
                          reads=[p, x, self.mT], writes=[x])
                fw.dma(fw.sp, self.xT_tile(t0, n), x[:, :, :n], x, reads=[x], writes=[self.xT])

    def ffn_up(self, l, hT):
        fw, T, L = self.fw, self.T, self.L
        W = L + 259
        c_ctx, c_lat = 1, 258
        NO = L + 257
        with self.phase():
            ws = [fw.sb("f_w%d" % i, [128, NFC, 128], BF16) for i in range(3)]
            ug = fw.sb("f_ug", [128, W], BF16)
            uv = fw.sb("f_uv", [128, W], BF16)
            acc = fw.sb("f_acc", [128, W], F32)
            sgb = fw.sb("f_sg", [128, W], BF16)
            ob = fw.sb("f_o", [128, T], BF16)
            cw = fw.sb("f_cw", [128, 88, 4], F32)
            pm = [fw.ps("f_p%d" % i, [128, 512], F32) for i in range(6)]
            fw.dma(fw.sp, cw[:], self.fcw.t.ap()[l], cw, reads=[self.fcw], writes=[cw])
            fw.op(fw.pool, lambda e: e.memset(ug[:], 0.0), writes=[ug])
            fw.op(fw.pool, lambda e: e.memset(uv[:], 0.0), writes=[uv])
            wsrc = self.w_up.t.ap()[l].rearrange("(kc p) n -> p kc n", p=128)
            iw = 0
            ip = 0
            chs = [ch for i in range(44) for ch in (i, 44 + i)]

            def load_w(k):
                fw.dma(fw.pool, ws[k % 3][:], wsrc[:, :, chs[k] * 128:(chs[k] + 1) * 128], ws[k % 3], reads=[self.w_up], writes=[ws[k % 3]])

            load_w(0)
            load_w(1)
            for i in range(44):
                for part, (ch, ub) in enumerate(((i, ug), (44 + i, uv))):
                    w = ws[iw % 3]
                    if iw + 2 < len(chs):
                        load_w(iw + 2)
                    iw += 1
                    for (t0, n, c) in self.tiles:
                        p = pm[ip % 6]
                        ip += 1
                        for kc in range(NFC):
                            fw.op(fw.pe, lambda e: e.matmul(p[:, :n], w[:, kc, :], hT[:, kc, t0:t0 + n], start=(kc == 0), stop=(kc == NFC - 1)),
                                  reads=[w, hT], writes=[p])
                        col = c_ctx + t0 if c == 1 else c_lat + (t0 - CTX)
                        fw.op(fw.act, lambda e: e.copy(ub[:, col:col + n], p[:, :n]), reads=[p], writes=[ub])
                    fw.op(fw.dve, lambda e: e.tensor_scalar(acc[:, 1:1 + NO], ub[:, 0:NO], cw[:, ch, 0:1], cw[:, ch, 3:4], ALU.mult, ALU.add),
                          reads=[ub, cw], writes=[acc])
                    for k in range(1, 3):
                        fw.op(fw.dve, lambda e: e.scalar_tensor_tensor(acc[:, 1:1 + NO], ub[:, k:k + NO], cw[:, ch, k:k + 1], acc[:, 1:1 + NO], ALU.mult, ALU.add),
                              reads=[ub, cw, acc], writes=[acc])
                    if part == 0:
                        fw.op(fw.act, lambda e: e.activation(sgb[:, 1:1 + NO], acc[:, 1:1 + NO], AF.Silu), reads=[acc], writes=[sgb])
                    else:
                        fw.op(fw.pool, lambda e: e.tensor_tensor(ob[:, 0:CTX], sgb[:, c_ctx:c_ctx + CTX], acc[:, c_ctx:c_ctx + CTX], ALU.mult),
                              reads=[sgb, acc], writes=[ob])
                        fw.op(fw.pool, lambda e: e.tensor_tensor(ob[:, CTX:T], sgb[:, c_lat:c_lat + L], acc[:, c_lat:c_lat + L], ALU.mult),
                              reads=[sgb, acc], writes=[ob])
                        fw.dma(fw.sp, self.ffaT.t.ap()[i * 128:(i + 1) * 128, :], ob[:], ob, reads=[ob], writes=[self.ffaT])

    def ffn_down(self, l):
        fw, T = self.fw, self.T
        with self.phase():
            ws = [fw.sb("d_w%d" % i, [128, 44, 512], BF16) for i in range(2)]
            ai = [fw.sb("d_a%d" % i, [128, 44, 512], BF16) for i in range(2)]
            xi = [fw.sb("d_x%d" % i, [128, 4, 512], F32) for i in range(2)]
            pp = [fw.ps("d_p%d" % i, [128, 512], F32) for i in range(4)]
            wsrc = self.w_down.t.ap()[l].rearrange("(kc p) n -> p kc n", p=128)
            iters = [(og, t0, n, c) for og in range(4) for (t0, n, c) in self.tiles]

            def load_w(og):
                w = ws[og % 2]
                for q in range(4):
                    fw.dma(fw.pool, w[:, q * 11:(q + 1) * 11, :], wsrc[:, q * 11:(q + 1) * 11, og * 512:(og + 1) * 512], w, reads=[self.w_down], writes=[w])

            def load_ax(i):
                og, t0, n, c = iters[i]
                a = ai[i % 2]
                x = xi[i % 2]
                for q in range(4):
                    fw.dma(fw.sp, a[:, q * 11:(q + 1) * 11, :n], self.fm(self.ffaT, 11, t0, n, c0=q * 11), a, reads=[self.ffaT], writes=[a])
                fw.dma(fw.sp, x[:, :, :n], self.fm(self.xT, 4, t0, n, c0=og * 4), x, reads=[self.xT], writes=[x])

            load_w(0)
            load_ax(0)
            it = 0
            for i, (og, t0, n, c) in enumerate(iters):
                w, a, x = ws[og % 2], ai[i % 2], xi[i % 2]
                if i + 1 < len(iters):
                    load_ax(i + 1)
                if t0 == 0 and og + 1 < 4:
                    load_w(og + 1)
                for j in range(4):
                    oc = og * 4 + j
                    p = pp[it % 4]
                    it += 1
                    for kc in range(44):
                        fw.op(fw.pe, lambda e: e.matmul(p[:, :n], w[:, kc, j * 128:(j + 1) * 128], a[:, kc, :n], start=(kc == 0), stop=(kc == 43)),
                              reads=[w, a], writes=[p])
                    fw.op(fw.dve, lambda e: e.scalar_tensor_tensor(x[:, j, :n], p[:, :n], self.mT[:, 80 + oc, c:c + 1], x[:, j, :n], ALU.mult, ALU.add),
                          reads=[p, x, self.mT], writes=[x])
                fw.dma(fw.sp, self.fm(self.xT, 4, t0, n, c0=og * 4), x[:, :, :n], x, reads=[x], writes=[self.xT])

    def final_phase(self):
        fw, T = self.fw, self.T
        with self.phase():
            xi = fw.sb("z_x", [128, NFC, 512], F32)
            sq = fw.sb("z_sq", [128, NFC, 512], BF16)
            rt = fw.sb("z_rt", [128, 512], F32)
            rstd = fw.sb("z_rs", [128, 512], F32)
            fn = fw.sb("z_fn", [128, NFC], F32)
            hn = fw.sb("z_hn", [128, NFC, 512], F32)
            so = [fw.sb("z_o%d" % i, [128, D], F32) for i in range(2)]
            pss = fw.ps("z_ps", [128, 512], F32)
            pt = [fw.ps("z_p%d" % i, [128, 512], F32) for i in range(4)]
            fw.dma(fw.sp, fn[:], self.fnwT.t.ap(), fn, reads=[self.fnwT], writes=[fn])
            io = 0
            for (t0, n, c) in self.tiles:
                if c == 1:
                    continue
                fw.dma(fw.sp, xi[:, :, :n], self.xT_tile(t0, n), xi, reads=[self.xT], writes=[xi])
                self.rms_tile(xi, n, sq, pss, rt, rstd, D)
                for fc in range(NFC):
                    fw.op(fw.dve, lambda e: e.scalar_tensor_tensor(hn[:, fc, :n], xi[:, fc, :n], fn[:, fc:fc + 1], rstd[:, :n], ALU.mult, ALU.mult),
                          reads=[xi, rstd, fn], writes=[hn])
                for s in range(n // 128):
                    o = so[io % 2]
                    io += 1
                    for q in range(4):
                        p = pt[q]
                        for j in range(4):
                            fc = q * 4 + j
                            fw.op(fw.pe, lambda e: e.transpose(p[:, j * 128:(j + 1) * 128], hn[:, fc, s * 128:(s + 1) * 128], self.identf), reads=[hn], writes=[p])
                        if q % 2 == 0:
                            fw.op(fw.dve, lambda e: e.tensor_copy(o[:, q * 512:(q + 1) * 512], p[:]), reads=[p], writes=[o])
                        else:
                            fw.op(fw.act, lambda e: e.copy(o[:, q * 512:(q + 1) * 512], p[:]), reads=[p], writes=[o])
                    r0 = t0 - CTX + s * 128
                    fw.dma(fw.sp, self.out.t.ap()[r0:r0 + 128, :], o[:], o, reads=[o], writes=[self.out])


def _fmT(v):
    v = np.asarray(v, np.float32)
    n = v.shape[-1] // 128
    return np.ascontiguousarray(np.swapaxes(v.reshape(v.shape[:-1] + (n, 128)), -1, -2))


def _consts():
    c = np.zeros((128, 7, 128), np.float32)
    c[:, 0, :] = np.eye(128)
    c[:, 1, :] = 1.0
    rm = np.zeros((128, 128), np.float32)
    for d in range(128):
        if (d % 64) < 32:
            rm[d + 32, d] = -1.0
        else:
            rm[d - 32, d] = 1.0
    c[:, 2, :] = rm
    tri = (np.arange(128)[:, None] <= np.arange(128)[None, :]).astype(np.float32)
    c[:, 3, :] = tri
    c[:, 4, :] = tri.T
    c[:, 5, :] = (1.0 - tri) * -30000.0
    c[:, 6, :] = (1.0 - tri.T) * -30000.0
    return c


def _rope(L):
    T = CTX + L
    cos = np.ones((128, T), np.float32)
    sin = np.zeros((128, T), np.float32)
    pos = np.arange(L)
    row = (pos // 64).astype(np.float32)
    col = (pos % 64).astype(np.float32)
    inv = (10000.0 ** (-np.arange(0, 64, 2, dtype=np.float32) / 64)).astype(np.float32)
    for d in range(128):
        axis = d // 64
        f = d % 32
        ang = (row if axis == 0 else col) * inv[f]
        cos[d, CTX:] = np.cos(ang.astype(np.float32))
        sin[d, CTX:] = np.sin(ang.astype(np.float32))
    return cos, sin


def make_in_maps(inp, L, depth, batches):
    f = lambda k: np.asarray(inp[k], np.float32)
    dp = depth
    shared = {
        "w_mod": np.ascontiguousarray(f("w_mod")[:dp]),
        "b_modT": _fmT(f("b_mod")[:dp]),
        "nmixT": _fmT(f("norm_mix_w")[:dp]),
        "nffnT": _fmT(f("norm_ffn_w")[:dp]),
        "w_in": np.ascontiguousarray(f("w_in")[:dp]),
        "qkn": np.ascontiguousarray(np.stack([f("q_norm_w")[:dp], f("k_norm_w")[:dp]], axis=-1)),
        "w_attn_o": np.ascontiguousarray(f("w_attn_o")[:dp]),
        "ccw": np.ascontiguousarray(np.transpose(f("conf_conv_w")[:dp].reshape(dp, 31, 8, 128), (0, 3, 2, 1))),
        "ccb": np.ascontiguousarray(np.stack([_fmT(f("conf_conv_b")[:dp]), _fmT(f("conf_ln_w")[:dp]), _fmT(f("conf_ln_b")[:dp])], axis=-1)),
        "w_conf_o": np.ascontiguousarray(f("w_conf_o")[:dp]),
        "scw": np.ascontiguousarray(np.concatenate([np.transpose(f("ssm_conv_w")[:dp].reshape(dp, 5, 12, 128), (0, 3, 2, 1)),
                                                    _fmT(f("ssm_conv_b")[:dp])[..., None]], axis=-1)),
        "snw": _fmT(f("ssm_norm_w")[:dp]),
        "w_ssm_o": np.ascontiguousarray(f("w_ssm_o")[:dp]),
        "w_out": np.ascontiguousarray(f("w_out")[:dp]),
        "w_up": np.ascontiguousarray(f("ffn_w_up")[:dp]),
        "fcw": np.ascontiguousarray(np.concatenate([np.transpose(f("ffn_conv_w")[:dp].reshape(dp, 3, 88, 128), (0, 3, 2, 1)),
                                                    _fmT(f("ffn_conv_b")[:dp])[..., None]], axis=-1)),
        "w_down": np.ascontiguousarray(f("ffn_w_down")[:dp]),
        "fnwT": _fmT(f("final_norm_w")),
        "consts": _consts(),
    }
    srow = np.zeros((dp, 128, 4, 32), np.float32)
    srow[:, :, 0, :] = f("ssm_a_log")[:dp].reshape(dp, 1, 32)
    srow[:, :, 1, :] = f("ssm_dt_bias")[:dp].reshape(dp, 1, 32)
    srow[:, :, 2, :] = f("ssm_d")[:dp].reshape(dp, 1, 32)
    shared["srow"] = srow
    cos, sin = _rope(L)
    shared["cosT"], shared["sinT"] = cos, sin
    maps = []
    for b in batches:
        m = dict(shared)
        m["x"] = np.ascontiguousarray(f("x")[b, :L])
        m["ctx"] = np.ascontiguousarray(f("ctx")[b])
        m["cT"] = np.ascontiguousarray(np.stack([_fmT(f("c")[b]), _fmT(f("c_ctx"))], axis=-1))
        maps.append(m)
    return maps


_NC_CACHE = {}


def kernel(**inputs):
    L = 4096
    key = (L, DEPTH)
    if key not in _NC_CACHE:
        _NC_CACHE[key] = Builder(L, DEPTH).build()
    nc = _NC_CACHE[key]
    real = make_in_maps(inputs, L, DEPTH, [0, 1, 2, 3])
    work = [0, 1, 2, 3]
    maps = real + real
    res = run_bass_kernel_spmd(nc, maps, core_ids=list(range(8)))
    out = np.stack([np.asarray(res.results[cid]["out"], np.float32) for cid in work], axis=0)
    return out
```

```python
import math
from contextlib import ExitStack
import numpy as np
import concourse.bass as bass
import concourse.mybir as mybir
from concourse.bass_utils import run_bass_kernel_spmd

F32 = mybir.dt.float32
BF16 = mybir.dt.bfloat16
AF = mybir.ActivationFunctionType
ALU = mybir.AluOpType

D = 2048
NFC = 16
CTX = 256
DEPTH = 4
EPS = 1e-6
IN_W = 12320
DFF = 5632
ATT_SCALE = 128 ** -0.5


class Buf:
    __slots__ = ("name", "t", "last_w", "reads", "dsem", "dcnt")

    def __init__(self, name, t=None):
        self.name = name
        self.t = t
        self.last_w = None
        self.reads = {}
        self.dsem = None
        self.dcnt = 0

    def __getitem__(self, idx):
        return self.t[idx]


class Eng:
    def __init__(self, fw, name, eng, counted=True):
        self.name = name
        self.eng = eng
        self.sem = fw.nc.alloc_semaphore("sem_" + name) if counted else None
        self.count = 0
        self.seen = {}


class FW:
    def __init__(self, nc):
        self.nc = nc
        self.pe = Eng(self, "pe", nc.tensor)
        self.act = Eng(self, "act", nc.scalar)
        self.dve = Eng(self, "dve", nc.vector)
        self.pool = Eng(self, "pool", nc.gpsimd)
        self.sp = Eng(self, "sp", nc.sync, counted=False)
        self.engs = [self.pe, self.act, self.dve, self.pool, self.sp]
        self.dsems = []
        self.free_dsems = []
        self.n_wait = 0
        self.n_ins = 0
        self.stack = None
        self.uid = 0
        self.live = [[]]

    def sb(self, name, shape, dtype):
        self.uid += 1
        name = "%s_%d" % (name, self.uid)
        b = Buf(name, self.stack.enter_context(self.nc.sbuf_tensor(name, list(shape), dtype)))
        self.live[-1].append(b)
        return b

    def ps(self, name, shape, dtype=F32):
        self.uid += 1
        name = "%s_%d" % (name, self.uid)
        return Buf(name, self.stack.enter_context(self.nc.psum_tensor(name, list(shape), dtype)))

    def dram(self, name, shape, dtype, kind="Internal"):
        return Buf(name, self.nc.dram_tensor(name, list(shape), dtype, kind=kind))

    def _wait(self, E, tok):
        sem, val = tok
        k = id(sem)
        if E.seen.get(k, 0) >= val:
            return
        if sem is E.sem and E.name == "pe":
            return
        E.eng.wait_ge(sem, val)
        E.seen[k] = val
        self.n_wait += 1

    def _deps(self, E, reads, writes):
        for b in reads:
            if b.last_w is not None:
                self._wait(E, b.last_w)
        for b in writes:
            if b.last_w is not None:
                self._wait(E, b.last_w)
            for tok in b.reads.values():
                self._wait(E, tok)

    def _commit(self, tok, reads, writes):
        k = id(tok[0])
        for b in reads:
            b.reads[k] = tok
        for b in writes:
            b.last_w = tok
            b.reads = {}

    def op(self, E, fn, reads=(), writes=()):
        self._deps(E, reads, writes)
        ins = fn(E.eng)
        E.count += 1
        ins.then_inc(E.sem, 1)
        self._commit((E.sem, E.count), reads, writes)
        self.n_ins += 1
        return ins

    def dma(self, Q, out_ap, in_ap, sb, reads=(), writes=()):
        if sb.dsem is None:
            if self.free_dsems:
                sb.dsem = self.free_dsems.pop()
            else:
                sb.dsem = [self.nc.alloc_semaphore("ds_%d" % len(self.dsems)), 0]
                self.dsems.append(sb.dsem)
        ds = sb.dsem
        if sb.dcnt:
            self._wait(Q, (ds[0], 16 * ds[1]))
        self._deps(Q, reads, writes)
        ins = Q.eng.dma_start(out=out_ap, in_=in_ap)
        sb.dcnt += 1
        ds[1] += 1
        ins.then_inc(ds[0], 16)
        self._commit((ds[0], 16 * ds[1]), reads, writes)
        self.n_ins += 1
        return ins

    def barrier(self):
        toks = [(e.sem, e.count) for e in self.engs if e.sem is not None and e.count]
        toks += [(d[0], 16 * d[1]) for d in self.dsems if d[1]]
        for E in self.engs:
            for tok in toks:
                self._wait(E, tok)


class Builder:
    def __init__(self, L=4096, depth=DEPTH, debug=False):
        self.L = L
        self.T = CTX + L
        self.NCH = self.T // 128
        self.depth = depth
        self.debug = debug
        self.tiles = [(0, CTX, 1)] + [(CTX + 512 * i, 512, 0) for i in range(L // 512)]
        self.nc = bass.Bass("TRN2", target_bir_lowering=False)
        self.fw = FW(self.nc)

    def inp(self, name, shape, dtype=F32):
        return self.fw.dram(name, shape, dtype, kind="ExternalInput")

    def scratch(self, name, shape, dtype):
        return self.fw.dram(name, shape, dtype, kind="ExternalOutput" if self.debug else "Internal")

    def phase(self):
        b = self

        class _P:
            def __enter__(s):
                s.prev = b.fw.stack
                s.st = ExitStack()
                s.st.__enter__()
                b.fw.stack = s.st
                b.fw.live.append([])
                return s

            def __exit__(s, *a):
                b.fw.barrier()
                for bf in b.fw.live.pop():
                    if bf.dsem is not None:
                        b.fw.free_dsems.append(bf.dsem)
                        bf.dsem = None
                s.st.__exit__(*a)
                b.fw.stack = s.prev
                return False
        return _P()

    def build(self):
        fw, nc, L, T = self.fw, self.nc, self.L, self.T
        dp = self.depth
        self.x_in = self.inp("x", [L, D])
        self.ctx_in = self.inp("ctx", [CTX, D])
        self.cT_in = self.inp("cT", [128, NFC, 2])
        self.w_mod = self.inp("w_mod", [dp, D, 6 * D])
        self.b_modT = self.inp("b_modT", [dp, 128, 96])
        self.nmixT = self.inp("nmixT", [dp, 128, NFC])
        self.nffnT = self.inp("nffnT", [dp, 128, NFC])
        self.w_in = self.inp("w_in", [dp, D, IN_W])
        self.qkn = self.inp("qkn", [dp, 128, 2])
        self.w_attn_o = self.inp("w_attn_o", [dp, 1024, D])
        self.ccw = self.inp("ccw", [dp, 128, 8, 31])
        self.ccb = self.inp("ccb", [dp, 128, 8, 3])
        self.w_conf_o = self.inp("w_conf_o", [dp, 1024, D])
        self.scw = self.inp("scw", [dp, 128, 12, 6])
        self.srow = self.inp("srow", [dp, 128, 4, 32])
        self.snw = self.inp("snw", [dp, 128, 8])
        self.w_ssm_o = self.inp("w_ssm_o", [dp, 1024, D])
        self.w_out = self.inp("w_out", [dp, D, D])
        self.w_up = self.inp("w_up", [dp, D, 2 * DFF])
        self.fcw = self.inp("fcw", [dp, 128, 88, 4])
        self.w_down = self.inp("w_down", [dp, DFF, D])
        self.fnwT = self.inp("fnwT", [128, NFC])
        self.cosT = self.inp("cosT", [128, T])
        self.sinT = self.inp("sinT", [128, T])
        self.consts_in = self.inp("consts", [128, 7, 128])
        self.out = self.fw.dram("out", [L, D], F32, kind="ExternalOutput")

        S = self.scratch
        self.xT = S("xT", [D, T], F32)
        self.qT = S("qT", [1024, T], BF16)
        self.kT = S("kT", [256, T], BF16)
        self.v_tm = S("v_tm", [T, 256], BF16)
        self.attT = S("attT", [1024, T], BF16)
        self.cfa = S("cfa", [1024, T], BF16)
        self.cfg = S("cfg", [1024, T], BF16)
        self.confT = S("confT", [1024, T], BF16)
        self.cfv = S("cfv", [1024, T], BF16)
        self.xcv = S("xcv", [1536, T], BF16)
        self.z_tm = S("z_tm", [T, 1024], BF16)
        self.xbc = S("xbc", [1536, T], BF16)
        self.dtd = S("dtd", [T, 32], F32)
        self.gat = S("gat", [6144, T], BF16)
        self.ytmp = S("ytmp", [T, 1024], F32)
        self.ytf = S("ytf", [T, 1024], F32)
        self.ssyT = S("ssyT", [1024, T], BF16)
        self.mrgT = S("mrgT", [D, T], BF16)
        self.ffaT = S("ffaT", [DFF, T], BF16)

        with ExitStack() as top:
            fw.stack = top
            self.cf = fw.sb("cf", [128, 7, 128], F32)
            fw.dma(fw.sp, self.cf[:], self.consts_in.t.ap(), self.cf, reads=[self.consts_in], writes=[self.cf])
            self.cb = fw.sb("cb", [128, 7, 128], BF16)
            fw.op(fw.dve, lambda e: e.tensor_copy(self.cb[:], self.cf[:]), reads=[self.cf], writes=[self.cb])
            self.identf = self.cf[:, 0, :]
            self.ident = self.cb[:, 0, :]
            self.ones = self.cb[:, 1, :]
            self.onesf = self.cf[:, 1, :]
            self.Rm = self.cb[:, 2, :]
            self.epsc = fw.sb("epsc", [128, 1], F32)
            fw.op(fw.pool, lambda e: e.memset(self.epsc[:], EPS), writes=[self.epsc])
            self.mTs = [fw.sb("mT%d" % i, [128, 96, 2], F32) for i in range(2)]
            self.g1s = [fw.sb("g1%d" % i, [128, NFC, 2], F32) for i in range(2)]
            self.g2s = [fw.sb("g2%d" % i, [128, NFC, 2], F32) for i in range(2)]
            self.CONST = [self.cf, self.cb, self.epsc]
            fw.barrier()

            self.init_phase()
            for l in range(dp):
                self.mT, self.g1, self.g2 = self.mTs[l % 2], self.g1s[l % 2], self.g2s[l % 2]
                self.norm_and_proj(l, 0)
                with self.phase():
                    ad = self.adaln_steps(l + 1) if l + 1 < dp else []
                    h_ = (len(ad) * 3) // 5
                    self.attention(l, ad[:h_])
                    self.conformer(l)
                    self.ssd(l, ad[h_:])
                self.oproj(l)
                self.wout(l)
                self.norm_and_proj(l, 1)
                self.ffn_down(l)
            self.final_phase()
            fw.barrier()
        return nc

    def xT_tile(self, t0, n):
        return self.xT.t.ap().rearrange("(c p) t -> p c t", p=128)[:, :, t0:t0 + n]

    def fm(self, buf, nchunk, t0, n, c0=0):
        return buf.t.ap().rearrange("(c p) t -> p c t", p=128)[:, c0:c0 + nchunk, t0:t0 + n]

    def init_phase(self):
        fw = self.fw
        with self.phase():
            xin = [fw.sb("i_x%d" % i, [128, D], F32) for i in range(2)]
            st = [fw.sb("i_s%d" % i, [128, NFC, 128], F32) for i in range(2)]
            pt = [fw.ps("i_p%d" % i, [128, 512], F32) for i in range(4)]
            side = self.adaln_steps(0)
            for tc in range(self.NCH):
                if side:
                    side.pop(0)()
                xi, so = xin[tc % 2], st[tc % 2]
                src = self.ctx_in if tc < 2 else self.x_in
                r0 = tc * 128 if tc < 2 else (tc - 2) * 128
                fw.dma(fw.sp, xi[:], src.t.ap()[r0:r0 + 128, :], xi, reads=[src], writes=[xi])
                for q in range(4):
                    p = pt[q]
                    for j in range(4):
                        fc = q * 4 + j
                        fw.op(fw.pe, lambda e: e.transpose(p[:, j * 128:(j + 1) * 128], xi[:, fc * 128:(fc + 1) * 128], self.identf),
                              reads=[xi], writes=[p])
                    eng = fw.dve if q % 2 == 0 else fw.act
                    if q % 2 == 0:
                        fw.op(fw.dve, lambda e: e.tensor_copy(so[:, q * 4:(q + 1) * 4, :], p[:].rearrange("p (a b) -> p a b", a=4)),
                              reads=[p], writes=[so])
                    else:
                        fw.op(fw.act, lambda e: e.copy(so[:, q * 4:(q + 1) * 4, :], p[:].rearrange("p (a b) -> p a b", a=4)),
                              reads=[p], writes=[so])
                fw.dma(fw.sp, self.xT_tile(tc * 128, 128), so[:], so, reads=[so], writes=[self.xT])
            while side:
                side.pop(0)()

    def adaln_steps(self, l):
        fw = self.fw
        mT, g1, g2 = self.mTs[l % 2], self.g1s[l % 2], self.g2s[l % 2]
        cT = fw.sb("a_c", [128, NFC, 2], F32)
        sc = fw.sb("a_sc", [128, NFC, 2], BF16)
        bm = fw.sb("a_bm", [128, 96], F32)
        nw = fw.sb("a_nw", [128, 2, NFC], F32)
        ws = [fw.sb("a_w%d" % i, [128, NFC, 512], BF16) for i in range(2)]
        pm = fw.ps("a_pm", [128, 96, 2], F32)
        wsrc = self.w_mod.t.ap()[l].rearrange("(kc p) n -> p kc n", p=128)

        def load(g):
            fw.dma(fw.pool, ws[g % 2][:], wsrc[:, :, g * 512:(g + 1) * 512], ws[g % 2], reads=[self.w_mod], writes=[ws[g % 2]])

        def setup():
            fw.dma(fw.sp, cT[:], self.cT_in.t.ap(), cT, reads=[self.cT_in], writes=[cT])
            fw.dma(fw.sp, bm[:], self.b_modT.t.ap()[l], bm, reads=[self.b_modT], writes=[bm])
            fw.dma(fw.sp, nw[:, 0, :], self.nmixT.t.ap()[l], nw, reads=[self.nmixT], writes=[nw])
            fw.dma(fw.sp, nw[:, 1, :], self.nffnT.t.ap()[l], nw, reads=[self.nffnT], writes=[nw])
            fw.op(fw.act, lambda e: e.activation(sc[:], cT[:], AF.Silu), reads=[cT], writes=[sc])
            load(0)

        def step(g):
            def f():
                w = ws[g % 2]
                if g + 1 < 24:
                    load(g + 1)
                for j in range(4):
                    oc = g * 4 + j
                    for kc in range(NFC):
                        fw.op(fw.pe, lambda e: e.matmul(pm[:, oc, :], w[:, kc, j * 128:(j + 1) * 128], sc[:, kc, :],
                                                        start=(kc == 0), stop=(kc == NFC - 1)),
                              reads=[w, sc], writes=[pm])
            return f

        def finish():
            for c in range(2):
                fw.op(fw.dve, lambda e: e.tensor_tensor(mT[:, :, c], pm[:, :, c], bm[:], ALU.add), reads=[pm, bm], writes=[mT])
            for c in range(2):
                fw.op(fw.dve, lambda e: e.scalar_tensor_tensor(g1[:, :, c], mT[:, 16:32, c], 1.0, nw[:, 0, :], ALU.add, ALU.mult),
                      reads=[mT, nw], writes=[g1])
                fw.op(fw.dve, lambda e: e.scalar_tensor_tensor(g2[:, :, c], mT[:, 64:80, c], 1.0, nw[:, 1, :], ALU.add, ALU.mult),
                      reads=[mT, nw], writes=[g2])

        return [setup] + [step(g) for g in range(24)] + [finish]

    def conf_conv_steps(self, l):
        fw, T, L = self.fw, self.T, self.L
        W = L + 301
        c_ctx, c_lat = 15, 286
        NO = L + 271
        a = fw.sb("cf_a", [128, T], BF16)
        sg = fw.sb("cf_g", [128, T], BF16)
        U = fw.sb("cf_U", [128, W], BF16)
        acc = fw.sb("cf_acc", [128, W], F32)
        vo = fw.sb("cf_vo", [128, T], BF16)
        cw = fw.sb("cf_w", [128, 8, 31], F32)
        cbb = fw.sb("cf_b", [128, 8, 3], F32)
        steps = []

        def setup():
            fw.dma(fw.sp, cw[:], self.ccw.t.ap()[l], cw, reads=[self.ccw], writes=[cw])
            fw.dma(fw.sp, cbb[:], self.ccb.t.ap()[l], cbb, reads=[self.ccb], writes=[cbb])
            fw.op(fw.pool, lambda e: e.memset(U[:], 0.0), writes=[U])
        steps.append(setup)
        for cc in range(8):
            def s0(cc=cc):
                fw.dma(fw.sp, a[:], self.cfa.t.ap()[cc * 128:(cc + 1) * 128, :], a, reads=[self.cfa], writes=[a])
                fw.dma(fw.sp, sg[:], self.cfg.t.ap()[cc * 128:(cc + 1) * 128, :], sg, reads=[self.cfg], writes=[sg])
                fw.op(fw.pool, lambda e: e.tensor_tensor(U[:, c_ctx:c_ctx + CTX], a[:, 0:CTX], sg[:, 0:CTX], ALU.mult), reads=[a, sg], writes=[U])
                fw.op(fw.pool, lambda e: e.tensor_tensor(U[:, c_lat:c_lat + L], a[:, CTX:T], sg[:, CTX:T], ALU.mult), reads=[a, sg], writes=[U])
                fw.op(fw.dve, lambda e: e.tensor_scalar(acc[:, 15:15 + NO], U[:, 0:NO], cw[:, cc, 0:1], cbb[:, cc, 0:1], ALU.mult, ALU.add),
                      reads=[U, cw, cbb], writes=[acc])
            steps.append(s0)
            for k0 in range(1, 31, 3):
                def st(cc=cc, k0=k0):
                    for k in range(k0, min(k0 + 3, 31)):
                        fw.op(fw.dve, lambda e: e.scalar_tensor_tensor(acc[:, 15:15 + NO], U[:, k:k + NO], cw[:, cc, k:k + 1], acc[:, 15:15 + NO], ALU.mult, ALU.add),
                              reads=[U, cw, acc], writes=[acc])
                steps.append(st)

            def s9(cc=cc):
                fw.op(fw.pool, lambda e: e.tensor_copy(vo[:, 0:CTX], acc[:, c_ctx:c_ctx + CTX]), reads=[acc], writes=[vo])
                fw.op(fw.pool, lambda e: e.tensor_copy(vo[:, CTX:T], acc[:, c_lat:c_lat + L]), reads=[acc], writes=[vo])
                fw.dma(fw.pool, self.cfv.t.ap()[cc * 128:(cc + 1) * 128, :], vo[:], vo, reads=[vo], writes=[self.cfv])
            steps.append(s9)
        return steps

    def rms_tile(self, xi, n, sq, pss, rt, rstd, width):
        fw = self.fw
        fw.op(fw.act, lambda e: e.activation(sq[:, :, :n], xi[:, :, :n], AF.Square), reads=[xi], writes=[sq])
        for fc in range(NFC):
            fw.op(fw.pe, lambda e: e.matmul(pss[:, :n], self.ones, sq[:, fc, :n], start=(fc == 0), stop=(fc == NFC - 1)),
                  reads=[sq], writes=[pss])
        fw.op(fw.act, lambda e: e.activation(rt[:, :n], pss[:, :n], AF.Sqrt, bias=self.epsc[:, 0:1], scale=1.0 / width),
              reads=[pss], writes=[rt])
        fw.op(fw.dve, lambda e: e.reciprocal(rstd[:, :n], rt[:, :n]), reads=[rt], writes=[rstd])

    def norm_and_proj(self, l, which):
        fw, T = self.fw, self.T
        with self.phase():
            hT = fw.sb("hT", [128, NFC, T], BF16)
            gsc = self.g1 if which == 0 else self.g2
            shift_off = 0 if which == 0 else 48
            with self.phase():
                xis = [fw.sb("n_x%d" % i, [128, NFC, 256], F32) for i in range(2)]
                sq = fw.sb("n_sq", [128, NFC, 256], BF16)
                rt = fw.sb("n_rt", [128, 256], F32)
                rstd = fw.sb("n_rs", [128, 256], F32)
                tmp = [fw.sb("n_t%d" % i, [128, 256], F32) for i in range(2)]
                pss = fw.ps("n_ps", [128, 512], F32)
                subs = [(t0 + o_, 256, c) for (t0, n, c) in self.tiles for o_ in range(0, n, 256)]

                def load(i):
                    t0, n, c = subs[i]
                    fw.dma(fw.sp, xis[i % 2][:, :, :n], self.xT_tile(t0, n), xis[i % 2], reads=[self.xT], writes=[xis[i % 2]])

                load(0)
                for i, (t0, n, c) in enumerate(subs):
                    xi = xis[i % 2]
                    if i + 1 < len(subs):
                        load(i + 1)
                    self.rms_tile(xi, n, sq, pss, rt, rstd, D)
                    for fc in range(NFC):
                        tm = tmp[fc % 2]
                        fw.op(fw.dve, lambda e: e.scalar_tensor_tensor(tm[:, :n], xi[:, fc, :n], gsc[:, fc, c:c + 1], rstd[:, :n], ALU.mult, ALU.mult),
                              reads=[xi, rstd, gsc], writes=[tm])
                        fw.op(fw.act, lambda e: e.activation(hT[:, fc, t0:t0 + n], tm[:, :n], AF.Identity,
                                                             bias=self.mT[:, shift_off + fc, c:c + 1], scale=1.0),
                              reads=[tm, self.mT], writes=[hT])
            if which == 0:
                self.win(l, hT)
            else:
                self.ffn_up(l, hT)

    def win(self, l, hT):
        fw, T, L = self.fw, self.T, self.L
        with self.phase():
            ws = [fw.sb("w_w%d" % i, [128, NFC, 256], BF16) for i in range(2)]
            ob = [fw.sb("w_o%d" % i, [128, T], BF16) for i in range(2)]
            pm = [fw.ps("w_p%d" % i, [128, 512], F32) for i in range(4)]
            pss = fw.ps("w_pss", [128, 512], F32)
            prot = fw.ps("w_prot", [128, 512], F32)
            ptms = [fw.ps("w_ptm%d" % i, [128, 512], F32) for i in range(2)]
            sq = fw.sb("w_sq", [128, 512], BF16)
            rt = fw.sb("w_rt", [128, 512], F32)
            rstd = fw.sb("w_rs", [128, 512], F32)
            qn = fw.sb("w_qn", [128, 512], BF16)
            css = [fw.sb("w_cs%d" % i, [128, 2, 512], F32) for i in range(3)]
            t1 = fw.sb("w_t1", [128, 512], F32)
            t2 = fw.sb("w_t2", [128, 512], F32)
            qk = fw.sb("w_qk", [128, 2], F32)
            vst = [fw.sb("w_vs%d" % i, [128, 512], BF16) for i in range(2)]
            dst = fw.sb("w_ds", [128, 32], F32)
            dst2 = fw.sb("w_ds2", [128, 32], F32)
            srow = fw.sb("w_srow", [128, 4, 32], F32)
            wdt = fw.sb("w_wdt", [128, NFC, 32], BF16)
            fw.dma(fw.sp, qk[:], self.qkn.t.ap()[l], qk, reads=[self.qkn], writes=[qk])
            fw.dma(fw.sp, srow[:], self.srow.t.ap()[l], srow, reads=[self.srow], writes=[srow])
            wsrc = self.w_in.t.ap()[l].rearrange("(kc p) n -> p kc n", p=128)
            cnt = {"w": 0, "o": 0, "p": 0, "v": 0}

            def load_w(c0, ncol):
                w = ws[cnt["w"] % 2]
                cnt["w"] += 1
                fw.dma(fw.pool, w[:, :, :ncol], wsrc[:, :, c0:c0 + ncol], w, reads=[self.w_in], writes=[w])
                return w

            def fm_chunk(w, j, post, dest, dchunk):
                o = ob[cnt["o"] % 2]
                cnt["o"] += 1
                for (t0, n, c) in self.tiles:
                    p = pm[cnt["p"] % 4]
                    cnt["p"] += 1
                    for kc in range(NFC):
                        fw.op(fw.pe, lambda e: e.matmul(p[:, :n], w[:, kc, j * 128:(j + 1) * 128], hT[:, kc, t0:t0 + n],
                                                        start=(kc == 0), stop=(kc == NFC - 1)),
                              reads=[w, hT], writes=[p])
                    post(p, o, t0, n)
                fw.dma(fw.sp, dest.t.ap()[dchunk * 128:(dchunk + 1) * 128, :], o[:], o, reads=[o], writes=[dest])

            def post_act(func):
                def f(p, o, t0, n):
                    fw.op(fw.act, lambda e: e.activation(o[:, t0:t0 + n], p[:, :n], func), reads=[p], writes=[o])
                return f

            def fm_chunk_qk(w, j, col, dest, dchunk):
                o = ob[cnt["o"] % 2]
                cnt["o"] += 1
                tl = self.tiles
                ps_of = {}

                def A(i):
                    t0, n, c = tl[i]
                    p = ps_of[i]
                    csb = css[i % 3]
                    fw.dma(fw.sp, csb[:, 0, :n], self.cosT.t.ap()[:, t0:t0 + n], csb, reads=[self.cosT], writes=[csb])
                    fw.dma(fw.sp, csb[:, 1, :n], self.sinT.t.ap()[:, t0:t0 + n], csb, reads=[self.sinT], writes=[csb])
                    fw.op(fw.act, lambda e: e.activation(sq[:, :n], p[:, :n], AF.Square), reads=[p], writes=[sq])

                def B(i):
                    t0, n, c = tl[i]
                    p = ps_of[i]
                    fw.op(fw.pe, lambda e: e.matmul(pss[:, :n], self.ones, sq[:, :n], start=True, stop=True), reads=[sq], writes=[pss])
                    fw.op(fw.act, lambda e: e.activation(rt[:, :n], pss[:, :n], AF.Sqrt, bias=self.epsc[:, 0:1], scale=1.0 / 128),
                          reads=[pss], writes=[rt])
                    fw.op(fw.dve, lambda e: e.reciprocal(rstd[:, :n], rt[:, :n]), reads=[rt], writes=[rstd])
                    fw.op(fw.dve, lambda e: e.scalar_tensor_tensor(qn[:, :n], p[:, :n], qk[:, col:col + 1], rstd[:, :n], ALU.mult, ALU.mult),
                          reads=[p, rstd, qk], writes=[qn])

                def C(i):
                    t0, n, c = tl[i]
                    csb = css[i % 3]
                    fw.op(fw.pe, lambda e: e.matmul(prot[:, :n], self.Rm, qn[:, :n], start=True, stop=True), reads=[qn], writes=[prot])
                    fw.op(fw.pool, lambda e: e.tensor_tensor(t1[:, :n], qn[:, :n], csb[:, 0, :n], ALU.mult), reads=[qn, csb], writes=[t1])
                    fw.op(fw.dve, lambda e: e.tensor_tensor(t2[:, :n], prot[:, :n], csb[:, 1, :n], ALU.mult), reads=[prot, csb], writes=[t2])
                    fw.op(fw.pool, lambda e: e.tensor_tensor(o[:, t0:t0 + n], t1[:, :n], t2[:, :n], ALU.add), reads=[t1, t2], writes=[o])

                nt = len(tl)
                for i in range(nt + 2):
                    if i < nt:
                        t0, n, c = tl[i]
                        p = pm[cnt["p"] % 4]
                        cnt["p"] += 1
                        ps_of[i] = p
                        for kc in range(NFC):
                            fw.op(fw.pe, lambda e: e.matmul(p[:, :n], w[:, kc, j * 128:(j + 1) * 128], hT[:, kc, t0:t0 + n],
                                                            start=(kc == 0), stop=(kc == NFC - 1)),
                                  reads=[w, hT], writes=[p])
                    if 0 <= i - 2 < nt:
                        C(i - 2)
                    if 0 <= i - 1 < nt:
                        B(i - 1)
                    if i < nt:
                        A(i)
                fw.dma(fw.sp, dest.t.ap()[dchunk * 128:(dchunk + 1) * 128, :], o[:], o, reads=[o], writes=[dest])

            def tm_chunk(w, c_lo, ncol, dest, dcol0, func):
                for tc in range(self.NCH):
                    ptm = ptms[tc % 2]
                    for kc in range(NFC):
                        fw.op(fw.pe, lambda e: e.matmul(ptm[:, :ncol], hT[:, kc, tc * 128:(tc + 1) * 128], w[:, kc, c_lo:c_lo + ncol],
                                                        start=(kc == 0), stop=(kc == NFC - 1)),
                              reads=[w, hT], writes=[ptm])
                    vs = vst[cnt["v"] % 2]
                    cnt["v"] += 1
                    fw.op(fw.act, lambda e: e.activation(vs[:, :ncol], ptm[:, :ncol], func), reads=[ptm], writes=[vs])
                    fw.dma(fw.sp, dest.t.ap()[tc * 128:(tc + 1) * 128, dcol0:dcol0 + ncol], vs[:, :ncol], vs, reads=[vs], writes=[dest])

            jobs = []

            def J(c0, fn):
                jobs.append((c0, fn))

            for s_ in range(5):
                def f(w, s_=s_):
                    for j in range(2):
                        oc = s_ * 2 + j
                        if oc < 8:
                            fm_chunk_qk(w, j, 0, self.qT, oc)
                        else:
                            fm_chunk_qk(w, j, 1, self.kT, oc - 8)
                J(s_ * 256, f)
            J(1280, lambda w: tm_chunk(w, 0, 256, self.v_tm, 0, AF.Copy))
            for s_ in range(4):
                def f(w, s_=s_):
                    for j in range(2):
                        fm_chunk(w, j, post_act(AF.Copy), self.cfa, s_ * 2 + j)
                J(1536 + s_ * 256, f)
            for s_ in range(4):
                def f(w, s_=s_):
                    for j in range(2):
                        fm_chunk(w, j, post_act(AF.Sigmoid), self.cfg, s_ * 2 + j)
                J(2560 + s_ * 256, f)
            for s_ in range(4):
                J(3584 + s_ * 256, lambda w, s_=s_: tm_chunk(w, 0, 256, self.z_tm, s_ * 256, AF.Silu))
            for s_ in range(6):
                def f(w, s_=s_):
                    for j in range(2):
                        fm_chunk(w, j, post_act(AF.Copy), self.xbc, s_ * 2 + j)
                J(4608 + s_ * 256, f)
            for s_ in range(24):
                def f(w, s_=s_):
                    for j in range(2):
                        fm_chunk(w, j, post_act(AF.Sigmoid), self.gat, s_ * 2 + j)
                J(6176 + s_ * 256, f)

            fw.dma(fw.pool, wdt[:], wsrc[:, :, 6144:6176], wdt, reads=[self.w_in], writes=[wdt])
            wcur = load_w(jobs[0][0], 256)
            for tc in range(self.NCH):
                ptm = ptms[tc % 2]
                for kc in range(NFC):
                    fw.op(fw.pe, lambda e: e.matmul(ptm[:, :32], hT[:, kc, tc * 128:(tc + 1) * 128], wdt[:, kc, :],
                                                    start=(kc == 0), stop=(kc == NFC - 1)),
                          reads=[wdt, hT], writes=[ptm])
                fw.op(fw.dve, lambda e: e.tensor_tensor(dst[:], ptm[:, :32], srow[:, 1, :], ALU.add), reads=[ptm, srow], writes=[dst])
                fw.op(fw.act, lambda e: e.activation(dst[:], dst[:], AF.Exp), reads=[dst], writes=[dst])
                fw.op(fw.dve, lambda e: e.tensor_scalar(dst2[:], dst[:], 1.0, None, ALU.add), reads=[dst], writes=[dst2])
                fw.op(fw.act, lambda e: e.activation(dst2[:], dst2[:], AF.Ln), reads=[dst2], writes=[dst2])
                fw.dma(fw.sp, self.dtd.t.ap()[tc * 128:(tc + 1) * 128, :], dst2[:], dst2, reads=[dst2], writes=[self.dtd])
            for i, (c0, fn) in enumerate(jobs):
                wnext = load_w(jobs[i + 1][0], 256) if i + 1 < len(jobs) else None
                fn(wcur)
                wcur = wnext

    def attention(self, l, ad=()):
        fw, T = self.fw, self.T
        NCH = self.NCH
        with self.phase():
            kT = fw.sb("at_k", [128, 2, T], BF16)
            vt = fw.sb("at_v", [128, NCH, 256], BF16)
            qt = [fw.sb("at_q%d" % i, [128, 512], BF16) for i in range(2)]
            pT = [fw.sb("at_p%d" % i, [128, 512], BF16) for i in range(3)]
            rs = fw.sb("at_rs", [128, 512], F32)
            ob = [fw.sb("at_o%d" % i, [128, 512], BF16) for i in range(2)]
            ps_s = [fw.ps("at_s%d" % i, [128, 512], F32) for i in range(3)]
            ps_o = [fw.ps("at_a%d" % i, [128, 512], F32) for i in range(2)]
            ps_r = [fw.ps("at_r%d" % i, [128, 512], F32) for i in range(2)]
            fw.dma(fw.sp, kT[:], self.fm(self.kT, 2, 0, T), kT, reads=[self.kT], writes=[kT])
            fw.dma(fw.sp, vt[:], self.v_tm.t.ap().rearrange("(c p) d -> p c d", p=128), vt, reads=[self.v_tm], writes=[vt])
            side = self.conf_conv_steps(l)
            sc_ = self.ssd_conv_steps(l)
            k_ = max(1, len(side) // len(sc_))
            mix = []
            while side or sc_:
                for _ in range(k_):
                    if side:
                        mix.append(side.pop(0))
                if sc_:
                    mix.append(sc_.pop(0))
            side = mix
            ad = list(ad)
            if ad:
                merged = []
                for i in range(max(len(side), len(ad))):
                    if i < len(side):
                        merged.append(side[i])
                    if i < len(ad):
                        merged.append(ad[i])
                side = merged
            iters = [(h, t0, n, c) for h in range(8) for (t0, n, c) in self.tiles]

            def load_q(i):
                h, t0, n, c = iters[i]
                q = qt[i % 2]
                fw.dma(fw.sp, q[:, :n], self.qT.t.ap()[h * 128:(h + 1) * 128, t0:t0 + n], q, reads=[self.qT], writes=[q])

            load_q(0)
            ip = 0
            for it, (h, t0, n, c) in enumerate(iters):
                g = h // 4
                nk = 2 if c == 1 else NCH
                q = qt[it % 2]
                o = ob[it % 2]
                pso, psr = ps_o[it % 2], ps_r[it % 2]
                if it + 1 < len(iters):
                    load_q(it + 1)
                nside = (len(side) + len(iters) - 1) // len(iters)
                for _ in range(nside):
                    if side:
                        side.pop(0)()

                def S(kc):
                    pss = ps_s[(ip + kc) % 3]
                    fw.op(fw.pe, lambda e: e.matmul(pss[:, :n], kT[:, g, kc * 128:(kc + 1) * 128], q[:, :n], start=True, stop=True),
                          reads=[kT, q], writes=[pss])

                S(0)
                if nk > 1:
                    S(1)
                for kc in range(nk):
                    pss = ps_s[(ip + kc) % 3]
                    p = pT[(ip + kc) % 3]
                    fw.op(fw.act, lambda e: e.activation(p[:, :n], pss[:, :n], AF.Exp, scale=ATT_SCALE), reads=[pss], writes=[p])
                    if kc + 2 < nk:
                        S(kc + 2)
                    fw.op(fw.pe, lambda e: e.matmul(pso[:, :n], vt[:, kc, g * 128:(g + 1) * 128], p[:, :n], start=(kc == 0), stop=(kc == nk - 1)),
                          reads=[vt, p], writes=[pso])
                    fw.op(fw.pe, lambda e: e.matmul(psr[:, :n], self.ones, p[:, :n], start=(kc == 0), stop=(kc == nk - 1)),
                          reads=[p], writes=[psr])
                ip += nk
                fw.op(fw.dve, lambda e: e.reciprocal(rs[:, :n], psr[:, :n]), reads=[psr], writes=[rs])
                fw.op(fw.dve, lambda e: e.tensor_tensor(o[:, :n], pso[:, :n], rs[:, :n], ALU.mult), reads=[pso, rs], writes=[o])
                fw.dma(fw.sp, self.attT.t.ap()[h * 128:(h + 1) * 128, t0:t0 + n], o[:, :n], o, reads=[o], writes=[self.attT])
            while side:
                side.pop(0)()

    def conformer(self, l):
        fw, T, L = self.fw, self.T, self.L
        with self.phase():
            cbb = fw.sb("cf_b", [128, 8, 3], F32)
            fw.dma(fw.sp, cbb[:], self.ccb.t.ap()[l], cbb, reads=[self.ccb], writes=[cbb])
            vin = [fw.sb("cf_vi%d" % i, [128, 8, 512], BF16) for i in range(2)]
            vsq = fw.sb("cf_sq", [128, 8, 512], BF16)
            mean = fw.sb("cf_m", [128, 512], F32)
            msq = fw.sb("cf_m2", [128, 512], F32)
            var = fw.sb("cf_var", [128, 512], F32)
            rt = fw.sb("cf_rt", [128, 512], F32)
            rstd = fw.sb("cf_rs", [128, 512], F32)
            tt = [fw.sb("cf_t%d" % i, [128, 512], F32) for i in range(2)]
            ob = [fw.sb("cf_o%d" % i, [128, 8, 512], BF16) for i in range(2)]
            p1 = fw.ps("cf_p1", [128, 512], F32)
            p2 = fw.ps("cf_p2", [128, 512], F32)
            def load_v(ti):
                t0, n, c = self.tiles[ti]
                fw.dma(fw.sp, vin[ti % 2][:, :, :n], self.fm(self.cfv, 8, t0, n), vin[ti % 2], reads=[self.cfv], writes=[vin[ti % 2]])

            load_v(0)
            for ti, (t0, n, c) in enumerate(self.tiles):
                o = ob[ti % 2]
                vres = vin[ti % 2]
                if ti + 1 < len(self.tiles):
                    load_v(ti + 1)
                fw.op(fw.act, lambda e: e.activation(vsq[:, :, :n], vres[:, :, :n], AF.Square), reads=[vres], writes=[vsq])
                for cc in range(8):
                    fw.op(fw.pe, lambda e: e.matmul(p1[:, :n], self.ones, vres[:, cc, :n], start=(cc == 0), stop=(cc == 7)),
                          reads=[vres], writes=[p1])
                for cc in range(8):
                    fw.op(fw.pe, lambda e: e.matmul(p2[:, :n], self.ones, vsq[:, cc, :n], start=(cc == 0), stop=(cc == 7)),
                          reads=[vsq], writes=[p2])
                fw.op(fw.dve, lambda e: e.tensor_scalar(mean[:, :n], p1[:, :n], 1.0 / 1024, None, ALU.mult), reads=[p1], writes=[mean])
                fw.op(fw.dve, lambda e: e.tensor_tensor(msq[:, :n], mean[:, :n], mean[:, :n], ALU.mult), reads=[mean], writes=[msq])
                fw.op(fw.dve, lambda e: e.scalar_tensor_tensor(var[:, :n], p2[:, :n], 1.0 / 1024, msq[:, :n], ALU.mult, ALU.subtract),
                      reads=[p2, msq], writes=[var])
                fw.op(fw.act, lambda e: e.activation(rt[:, :n], var[:, :n], AF.Sqrt, bias=self.epsc[:, 0:1], scale=1.0), reads=[var], writes=[rt])
                fw.op(fw.dve, lambda e: e.reciprocal(rstd[:, :n], rt[:, :n]), reads=[rt], writes=[rstd])
                for cc in range(8):
                    t = tt[cc % 2]
                    fw.op(fw.dve, lambda e: e.tensor_tensor(t[:, :n], vres[:, cc, :n], mean[:, :n], ALU.subtract), reads=[vres, mean], writes=[t])
                    fw.op(fw.pool if cc % 2 else fw.dve, lambda e: e.tensor_tensor(t[:, :n], t[:, :n], rstd[:, :n], ALU.mult), reads=[t, rstd], writes=[t])
                    fw.op(fw.act, lambda e: e.activation(o[:, cc, :n], t[:, :n], AF.Silu, bias=cbb[:, cc, 2:3], scale=cbb[:, cc, 1:2]),
                          reads=[t, cbb], writes=[o])
                fw.dma(fw.sp, self.fm(self.confT, 8, t0, n), o[:, :, :n], o, reads=[o], writes=[self.confT])

    def ssd_conv_steps(self, l):
        fw, T, L = self.fw, self.T, self.L
        W = L + 262
        c_ctx, c_lat = 2, 260
        NO = L + 258
        xr = fw.sb("s_xr", [128, T], BF16)
        U = fw.sb("s_U", [128, W], BF16)
        acc = fw.sb("s_acc", [128, W], F32)
        xo = fw.sb("s_xo", [128, T], BF16)
        tmpc = fw.sb("s_tmpc", [128, W], F32)
        cw = fw.sb("s_cw", [128, 12, 6], F32)
        steps = []

        def setup():
            fw.dma(fw.sp, cw[:], self.scw.t.ap()[l], cw, reads=[self.scw], writes=[cw])
            fw.op(fw.pool, lambda e: e.memset(U[:], 0.0), writes=[U])
        steps.append(setup)
        for cc in range(12):
            def st(cc=cc):
                fw.dma(fw.sp, xr[:], self.xbc.t.ap()[cc * 128:(cc + 1) * 128, :], xr, reads=[self.xbc], writes=[xr])
                fw.op(fw.pool, lambda e: e.tensor_copy(U[:, c_ctx:c_ctx + CTX], xr[:, 0:CTX]), reads=[xr], writes=[U])
                fw.op(fw.pool, lambda e: e.tensor_copy(U[:, c_lat:c_lat + L], xr[:, CTX:T]), reads=[xr], writes=[U])
                fw.op(fw.pool, lambda e: e.tensor_scalar(acc[:, 2:2 + NO], U[:, 0:NO], cw[:, cc, 0:1], cw[:, cc, 5:6], ALU.mult, ALU.add),
                      reads=[U, cw], writes=[acc])
            steps.append(st)
            for k in range(1, 5):
                def stk(cc=cc, k=k):
                    fw.op(fw.pool, lambda e: e.tensor_scalar(tmpc[:, 0:NO], U[:, k:k + NO], cw[:, cc, k:k + 1], 0.0, ALU.mult, ALU.add),
                          reads=[U, cw], writes=[tmpc])
                    fw.op(fw.pool, lambda e: e.tensor_tensor(acc[:, 2:2 + NO], acc[:, 2:2 + NO], tmpc[:, 0:NO], ALU.add),
                          reads=[acc, tmpc], writes=[acc])
                steps.append(stk)

            def st2(cc=cc):
                fw.op(fw.act, lambda e: e.activation(xo[:, 0:CTX], acc[:, c_ctx:c_ctx + CTX], AF.Silu), reads=[acc], writes=[xo])
                fw.op(fw.act, lambda e: e.activation(xo[:, CTX:T], acc[:, c_lat:c_lat + L], AF.Silu), reads=[acc], writes=[xo])
                fw.dma(fw.pool, self.xcv.t.ap()[cc * 128:(cc + 1) * 128, :], xo[:], xo, reads=[xo], writes=[self.xcv])
            steps.append(st2)
        return steps

    def ssd(self, l, ad=()):
        fw, T, L = self.fw, self.T, self.L
        NCH = self.NCH
        W = L + 262
        c_ctx, c_lat = 2, 260
        NO = L + 258
        Mf = self.cf[:, 3, :]
        Mb = self.cf[:, 4, :]
        h16 = lambda ap, h=16: ap.rearrange("p (h d) -> p h d", h=h)
        with self.phase():
            srow = fw.sb("s_srow", [128, 4, 32], F32)
            nw = fw.sb("s_nw", [128, 8], F32)
            fw.dma(fw.sp, srow[:], self.srow.t.ap()[l], srow, reads=[self.srow], writes=[srow])
            fw.dma(fw.sp, nw[:], self.snw.t.ap()[l], nw, reads=[self.snw], writes=[nw])
            dt = fw.sb("s_dt", [128, NCH, 32], F32)
            aa = fw.sb("s_a", [128, NCH, 32], F32)
            acoef = fw.sb("s_ac", [128, 32], F32)
            dsum = fw.sb("s_dsum", [128, 16], F32)
            fw.dma(fw.sp, dt[:], self.dtd.t.ap().rearrange("(c p) h -> p c h", p=128), dt, reads=[self.dtd], writes=[dt])
            fw.op(fw.act, lambda e: e.activation(acoef[:], srow[:, 0, :], AF.Exp), reads=[srow], writes=[acoef])
            fw.op(fw.dve, lambda e: e.tensor_scalar(acoef[:], acoef[:], -1.0, None, ALU.mult), reads=[acoef], writes=[acoef])
            fw.op(fw.dve, lambda e: e.tensor_tensor(aa[:], dt[:], acoef[:].unsqueeze(1).broadcast_to([128, NCH, 32]), ALU.mult),
                  reads=[dt, acoef], writes=[aa])
            fw.op(fw.dve, lambda e: e.tensor_tensor(dsum[:], srow[:, 2, 0:16], srow[:, 2, 16:32], ALU.add), reads=[srow], writes=[dsum])

            with self.phase():
                p_xs = fw.ps("s_pxs", [128, 1024], BF16)
                p_sm = fw.ps("s_psm", [128, 512], F32)
                p_bt = fw.ps("s_pbt", [128, 256], BF16)
                p_R = fw.ps("s_pR", [128, 512], F32)
                p_y = [fw.ps("s_py%d" % i, [128, 512], F32) for i in range(2)]
                p_2 = fw.ps("s_p2", [128, 512], F32)

                def make_chain(d):
                    M = Mf if d == 0 else Mb
                    NM = self.cf[:, 5 + d, :]
                    sfx = "_%d" % d
                    two = lambda nm, shp, dt_: [fw.sb(nm + sfx + "_%d" % i, shp, dt_) for i in range(2)]
                    xcb = two("s_xcb", [128, 12, 128], BF16)
                    xs_tm = two("s_xs", [128, 1024], BF16)
                    b_tm = two("s_bt", [128, 256], BF16)
                    xw = two("s_xw", [128, 1024], BF16)
                    ein = two("s_ein", [128, 16], F32)
                    edec = two("s_edec", [128, 16], F32)
                    yI = two("s_yI", [128, 1024], F32)
                    nacum = fw.sb("s_nacum" + sfx, [128, 16], F32)
                    acum = fw.sb("s_acum" + sfx, [128, 16], F32)
                    tot = fw.sb("s_tot" + sfx, [128, 16], F32)
                    eout = fw.sb("s_eout" + sfx, [128, 16], F32)
                    xdt = fw.sb("s_xdt" + sfx, [128, 1024], BF16)
                    abc = fw.sb("s_abc" + sfx, [128, 2, 16, 128], BF16)
                    ahl = fw.sb("s_ahl" + sfx, [128, 2, 16], F32)
                    ah16 = fw.sb("s_ah16" + sfx, [128, 16], BF16)
                    Mb16 = self.cb[:, 3 + d, :]
                    NMb16 = self.cb[:, 5 + d, :]
                    seg = fw.sb("s_seg" + sfx, [128, 16, 128], F32)
                    mt = fw.sb("s_mt" + sfx, [128, 16, 128], BF16)
                    cbm = fw.sb("s_cbm" + sfx, [128, 2, 128], F32)
                    H = fw.sb("s_H" + sfx, [128, 2, 512], F32)
                    Hb = fw.sb("s_Hb" + sfx, [128, 2, 512], BF16)
                    yit = fw.sb("s_yit" + sfx, [128, 1024], F32)
                    gx = fw.sb("s_gx" + sfx, [128, 1024], F32) if d == 0 else None
                    ydst = self.ytmp if d == 1 else self.ytf
                    order = list(range(NCH)) if d == 0 else [1, 0] + list(range(NCH - 1, 1, -1))
                    n = len(order)

                    def load(k):
                        tc = order[k]
                        fw.dma(fw.sp, xcb[k % 2][:], self.fm(self.xcv, 12, tc * 128, 128), xcb[k % 2], reads=[self.xcv], writes=[xcb[k % 2]])

                    def init():
                        fw.op(fw.pool, lambda e: e.memset(H[:], 0.0), writes=[H])
                        fw.op(fw.pool, lambda e: e.memset(Hb[:], 0.0), writes=[Hb])
                        load(0)

                    def stageI(k):
                        tc = order[k]
                        p = k % 2
                        xc = xcb[p]
                        a_c = aa[:, tc, d * 16:(d + 1) * 16]
                        dt_c = dt[:, tc, d * 16:(d + 1) * 16]

                        def s1():
                            if k + 1 < n:
                                load(k + 1)
                            for j in range(8):
                                fw.op(fw.pe, lambda e: e.transpose(p_xs[:, j * 128:(j + 1) * 128], xc[:, j, :], self.ident), reads=[xc], writes=[p_xs])
                            for j in range(2):
                                fw.op(fw.pe, lambda e: e.transpose(p_bt[:, j * 128:(j + 1) * 128], xc[:, 8 + j, :], self.ident), reads=[xc], writes=[p_bt])
                            fw.op(fw.act, lambda e: e.copy(xs_tm[p][:], p_xs[:]), reads=[p_xs], writes=[xs_tm[p]])
                            fw.op(fw.act, lambda e: e.copy(b_tm[p][:], p_bt[:]), reads=[p_bt], writes=[b_tm[p]])
                            fw.op(fw.pe, lambda e: e.matmul(p_sm[:, 0:16], M, a_c, start=True, stop=True), reads=[aa], writes=[p_sm])
                            fw.op(fw.pe, lambda e: e.matmul(p_sm[:, 16:32], self.onesf, a_c, start=True, stop=True), reads=[aa], writes=[p_sm])
                            fw.op(fw.dve, lambda e: e.tensor_copy(acum[:], p_sm[:, 0:16]), reads=[p_sm], writes=[acum])
                            fw.op(fw.dve, lambda e: e.tensor_copy(tot[:], p_sm[:, 16:32]), reads=[p_sm], writes=[tot])
                            fw.op(fw.dve, lambda e: e.tensor_scalar(nacum[:], p_sm[:, 0:16], -1.0, None, ALU.mult), reads=[p_sm], writes=[nacum])
                            fw.op(fw.dve, lambda e: e.tensor_copy(ah16[:], a_c), reads=[aa], writes=[ah16])
                            fw.op(fw.dve, lambda e: e.tensor_copy(ahl[:, 0, :], ah16[:]), reads=[ah16], writes=[ahl])
                            fw.op(fw.dve, lambda e: e.tensor_tensor(ahl[:, 1, :], a_c, ahl[:, 0, :], ALU.subtract), reads=[aa, ahl], writes=[ahl])
                            fw.op(fw.act, lambda e: e.activation(abc[:].rearrange("p a h i -> p (a h) i"),
                                                                 ahl[:].rearrange("p a h -> p (a h)").unsqueeze(2).broadcast_to([128, 32, 128]), AF.Copy),
                                  reads=[ahl], writes=[abc])

                        def s2():
                            fw.op(fw.act, lambda e: e.activation(ein[p][:], acum[:], AF.Exp), reads=[acum], writes=[ein[p]])
                            fw.op(fw.act, lambda e: e.activation(edec[p][:], tot[:], AF.Exp), reads=[tot], writes=[edec[p]])
                            fw.op(fw.dve, lambda e: e.tensor_tensor(eout[:], tot[:], acum[:], ALU.subtract), reads=[tot, acum], writes=[eout])
                            fw.op(fw.act, lambda e: e.activation(eout[:], eout[:], AF.Exp), reads=[eout], writes=[eout])
                            fw.op(fw.dve, lambda e: e.tensor_tensor(h16(xdt[:]), h16(xs_tm[p][:]), dt_c.unsqueeze(2).broadcast_to([128, 16, 64]), ALU.mult),
                                  reads=[xs_tm[p], dt], writes=[xdt])
                            fw.op(fw.pool, lambda e: e.tensor_tensor(h16(xw[p][:]), h16(xdt[:]), eout[:].unsqueeze(2).broadcast_to([128, 16, 64]), ALU.mult),
                                  reads=[xdt, eout], writes=[xw[p]])

                        def s3():
                            for g in range(2):
                                fw.op(fw.pe, lambda e: e.matmul(p_sm[:, 128 + g * 128:256 + g * 128], xc[:, 8 + g, :], xc[:, 10 + g, :], start=True, stop=True),
                                      reads=[xc], writes=[p_sm])
                            fw.op(fw.dve, lambda e: e.tensor_tensor(cbm[:], p_sm[:, 128:384].rearrange("p (g i) -> p g i", g=2),
                                                                    M.unsqueeze(1).broadcast_to([128, 2, 128]), ALU.mult),
                                  reads=[p_sm], writes=[cbm])
                            for hq in range(4):
                                for j in range(4):
                                    h = hq * 4 + j
                                    fw.op(fw.pe, lambda e: e.matmul(p_R[:, j * 128:(j + 1) * 128], abc[:, 0, h, :], Mb16, start=True, stop=False),
                                          reads=[abc], writes=[p_R])
                                    fw.op(fw.pe, lambda e: e.matmul(p_R[:, j * 128:(j + 1) * 128], abc[:, 1, h, :], Mb16, start=False, stop=False),
                                          reads=[abc], writes=[p_R])
                                    fw.op(fw.pe, lambda e: e.matmul(p_R[:, j * 128:(j + 1) * 128], self.ident, NMb16, start=False, stop=True),
                                          reads=[abc], writes=[p_R])
                                for j in range(4):
                                    h = hq * 4 + j
                                    fw.op(fw.act, lambda e: e.activation(seg[:, h, :], p_R[:, j * 128:(j + 1) * 128], AF.Exp, bias=nacum[:, h:h + 1], scale=1.0),
                                          reads=[p_R, nacum], writes=[seg])

                        def s4():
                            for g in range(2):
                                fw.op(fw.dve, lambda e: e.tensor_tensor(mt[:, g * 8:(g + 1) * 8, :], seg[:, g * 8:(g + 1) * 8, :],
                                                                        cbm[:, g, :].unsqueeze(1).broadcast_to([128, 8, 128]), ALU.mult),
                                      reads=[seg, cbm], writes=[mt])

                        def s5():
                            for h in range(16):
                                g = h // 8
                                fw.op(fw.pe, lambda e: e.matmul(p_y[g][:, (h % 8) * 64:(h % 8 + 1) * 64], mt[:, h, :], xdt[:, h * 64:(h + 1) * 64], start=True, stop=True),
                                      reads=[mt, xdt], writes=[p_y[g]])
                            for g in range(2):
                                fw.op(fw.act, lambda e: e.copy(yI[p][:, g * 512:(g + 1) * 512], p_y[g][:]), reads=[p_y[g]], writes=[yI[p]])
                        return [s1, s2, s3, s4, s5]

                    def stageII(k):
                        tc = order[k]
                        p = k % 2
                        xc = xcb[p]

                        def t1():
                            for g in range(2):
                                for half in range(2):
                                    fw.op(fw.pe, lambda e: e.matmul(p_2[:, 0:256], xc[:, 10 + g, :], Hb[:, g, half * 256:(half + 1) * 256], start=True, stop=True),
                                          reads=[xc, Hb], writes=[p_2])
                                    h0 = g * 8 + half * 4
                                    fw.op(fw.dve, lambda e: e.tensor_tensor(h16(yit[:, h0 * 64:(h0 + 4) * 64], 4), h16(p_2[:, 0:256], 4),
                                                                            ein[p][:, h0:h0 + 4].unsqueeze(2).broadcast_to([128, 4, 64]), ALU.mult),
                                          reads=[p_2, ein[p]], writes=[yit])

                        def t2():
                            for g in range(2):
                                fw.op(fw.pe, lambda e: e.matmul(p_2[:], b_tm[p][:, g * 128:(g + 1) * 128], xw[p][:, g * 512:(g + 1) * 512], start=True, stop=True),
                                      reads=[b_tm[p], xw[p]], writes=[p_2])
                                fw.op(fw.pool, lambda e: e.tensor_tensor(h16(H[:, g, :], 8), h16(H[:, g, :], 8),
                                                                         edec[p][:, g * 8:(g + 1) * 8].unsqueeze(2).broadcast_to([128, 8, 64]), ALU.mult),
                                      reads=[H, edec[p]], writes=[H])
                                fw.op(fw.dve, lambda e: e.tensor_tensor(H[:, g, :], H[:, g, :], p_2[:], ALU.add), reads=[H, p_2], writes=[H])
                            fw.op(fw.act, lambda e: e.copy(Hb[:], H[:]), reads=[H], writes=[Hb])

                        def t3():
                            fw.op(fw.pool, lambda e: e.tensor_tensor(yI[p][:], yI[p][:], yit[:], ALU.add), reads=[yI[p], yit], writes=[yI[p]])
                            if d == 0:
                                fw.op(fw.dve, lambda e: e.tensor_tensor(h16(gx[:]), h16(xs_tm[p][:]), dsum[:].unsqueeze(2).broadcast_to([128, 16, 64]), ALU.mult),
                                      reads=[xs_tm[p], dsum], writes=[gx])
                                fw.op(fw.pool, lambda e: e.tensor_tensor(yI[p][:], yI[p][:], gx[:], ALU.add), reads=[yI[p], gx], writes=[yI[p]])
                            fw.dma(fw.sp, ydst.t.ap()[tc * 128:(tc + 1) * 128, :], yI[p][:], yI[p], reads=[yI[p]], writes=[ydst])
                        return [t1, t2, t3]

                    seq = [init] + stageI(0)
                    for k in range(1, n):
                        a_, b_ = stageI(k), stageII(k - 1)
                        for i in range(max(len(a_), len(b_))):
                            if i < len(b_):
                                seq.append(b_[i])
                            if i < len(a_):
                                seq.append(a_[i])
                    seq += stageII(n - 1)
                    return seq

                cf_ = make_chain(0)
                cb_ = make_chain(1)
                ad = list(ad)
                nn = max(len(cf_), len(cb_))
                every = max(1, (nn - 8) // max(1, len(ad)))
                for i in range(nn):
                    if ad and i % every == 0:
                        ad.pop(0)()
                    if i < len(cf_):
                        cf_[i]()
                    if i < len(cb_):
                        cb_[i]()
                while ad:
                    ad.pop(0)()

            with self.phase():
                yf = [fw.sb("s_yf%d" % i, [128, 1024], F32) for i in range(2)]
                yb = [fw.sb("s_yb%d" % i, [128, 1024], F32) for i in range(2)]
                zt = [fw.sb("s_z%d" % i, [128, 1024], BF16) for i in range(2)]
                g1 = [fw.sb("s_g1%d" % i, [128, 1024], F32) for i in range(2)]
                sqv = fw.sb("s_sqv", [128, 1024], F32)
                g2 = [fw.sb("s_g2%d" % i, [128, 1024], BF16) for i in range(2)]
                ss = [fw.sb("s_ss%d" % i, [128, 1], F32) for i in range(2)]
                rt = [fw.sb("s_rt%d" % i, [128, 1], F32) for i in range(2)]
                rstd = [fw.sb("s_rstd%d" % i, [128, 1], F32) for i in range(2)]
                yT = [fw.sb("s_yT%d" % i, [128, 8, 128], BF16) for i in range(2)]
                p_t = [fw.ps("s_pt%d" % i, [128, 1024], BF16) for i in range(2)]

                def load(tc):
                    k = tc % 2
                    fw.dma(fw.sp, yf[k][:], self.ytf.t.ap()[tc * 128:(tc + 1) * 128, :], yf[k], reads=[self.ytf], writes=[yf[k]])
                    fw.dma(fw.sp, yb[k][:], self.ytmp.t.ap()[tc * 128:(tc + 1) * 128, :], yb[k], reads=[self.ytmp], writes=[yb[k]])
                    fw.dma(fw.sp, zt[k][:], self.z_tm.t.ap()[tc * 128:(tc + 1) * 128, :], zt[k], reads=[self.z_tm], writes=[zt[k]])

                load(0)
                for tc in range(NCH):
                    k = tc % 2
                    if tc + 1 < NCH:
                        load(tc + 1)
                    fw.op(fw.dve, lambda e: e.tensor_tensor(yf[k][:], yf[k][:], yb[k][:], ALU.add), reads=[yf[k], yb[k]], writes=[yf[k]])
                    fw.op(fw.dve, lambda e: e.tensor_tensor(g1[k][:], yf[k][:], zt[k][:], ALU.mult), reads=[yf[k], zt[k]], writes=[g1[k]])
                    fw.op(fw.act, lambda e: e.activation(sqv[:], g1[k][:], AF.Square), reads=[g1[k]], writes=[sqv])
                    fw.op(fw.dve, lambda e: e.reduce_sum(ss[k][:], sqv[:], mybir.AxisListType.X), reads=[sqv], writes=[ss[k]])
                    fw.op(fw.act, lambda e: e.activation(rt[k][:], ss[k][:], AF.Sqrt, bias=self.epsc[:, 0:1], scale=1.0 / 1024), reads=[ss[k]], writes=[rt[k]])
                    fw.op(fw.dve, lambda e: e.reciprocal(rstd[k][:], rt[k][:]), reads=[rt[k]], writes=[rstd[k]])
                    fw.op(fw.act, lambda e: e.activation(g2[k][:], g1[k][:], AF.Copy, scale=rstd[k][:, 0:1]), reads=[g1[k], rstd[k]], writes=[g2[k]])
                    for j in range(8):
                        fw.op(fw.pe, lambda e: e.transpose(p_t[k][:, j * 128:(j + 1) * 128], g2[k][:, j * 128:(j + 1) * 128], self.ident), reads=[g2[k]], writes=[p_t[k]])
                    for j in range(8):
                        fw.op(fw.dve, lambda e: e.tensor_scalar(yT[k][:, j, :], p_t[k][:, j * 128:(j + 1) * 128], nw[:, j:j + 1], None, ALU.mult),
                              reads=[p_t[k], nw], writes=[yT[k]])
                    fw.dma(fw.sp, self.fm(self.ssyT, 8, tc * 128, 128), yT[k][:], yT[k], reads=[yT[k]], writes=[self.ssyT])

    def oproj(self, l):
        fw, T = self.fw, self.T
        with self.phase():
            wo = [fw.sb("o_w%d" % i, [128, 8, D], BF16) for i in range(3)]
            srcs = [self.w_attn_o, self.w_conf_o, self.w_ssm_o]
            acts = [self.attT, self.confT, self.ssyT]
            for b in range(3):
                wsrc = srcs[b].t.ap()[l].rearrange("(kc p) n -> p kc n", p=128)
                for q in range(4):
                    fw.dma(fw.pool, wo[b][:, :, q * 512:(q + 1) * 512], wsrc[:, :, q * 512:(q + 1) * 512], wo[b], reads=[srcs[b]], writes=[wo[b]])
            xin = [[fw.sb("o_x%d_%d" % (i, b), [128, 8, 512], BF16) for b in range(3)] for i in range(2)]
            gts = [fw.sb("o_g%d" % i, [128, 3, 512], BF16) for i in range(3)]
            mo = [fw.sb("o_m%d" % i, [128, NFC, 512], BF16) for i in range(2)]
            tt = [[fw.sb("o_t%d_%d" % (i, j), [128, 512], F32) for j in range(3)] for i in range(2)]
            pp = [[fw.ps("o_p%d_%d" % (i, j), [128, 512], F32) for j in range(3)] for i in range(2)]
            iters = [(ti, oc) for ti in range(len(self.tiles)) for oc in range(NFC)]

            def load_x(ti):
                t0, n, c = self.tiles[ti]
                for b in range(3):
                    xb = xin[ti % 2][b]
                    fw.dma(fw.sp, xb[:, :, :n], self.fm(acts[b], 8, t0, n), xb, reads=[acts[b]], writes=[xb])

            def load_g(i):
                ti, oc = iters[i]
                t0, n, c = self.tiles[ti]
                gt = gts[i % 3]
                src = self.gat.t.ap().rearrange("(b c p) t -> p b c t", p=128, b=3)[:, :, oc, t0:t0 + n]
                fw.dma(fw.sp, gt[:, :, :n], src, gt, reads=[self.gat], writes=[gt])

            load_x(0)
            load_g(0)
            load_g(1)
            for i, (ti, oc) in enumerate(iters):
                t0, n, c = self.tiles[ti]
                if oc == 0 and ti + 1 < len(self.tiles):
                    load_x(ti + 1)
                if i + 2 < len(iters):
                    load_g(i + 2)
                P, Tt, gt = pp[i % 2], tt[i % 2], gts[i % 3]
                xs = xin[ti % 2]
                for b in range(3):
                    for kc in range(8):
                        fw.op(fw.pe, lambda e: e.matmul(P[b][:, :n], wo[b][:, kc, oc * 128:(oc + 1) * 128], xs[b][:, kc, :n], start=(kc == 0), stop=(kc == 7)),
                              reads=[wo[b], xs[b]], writes=[P[b]])
                    fw.op(fw.dve, lambda e: e.tensor_tensor(Tt[b][:, :n], P[b][:, :n], gt[:, b, :n], ALU.mult), reads=[P[b], gt], writes=[Tt[b]])
                fw.op(fw.pool, lambda e: e.tensor_tensor(Tt[0][:, :n], Tt[0][:, :n], Tt[1][:, :n], ALU.add), reads=[Tt[0], Tt[1]], writes=[Tt[0]])
                fw.op(fw.pool, lambda e: e.tensor_tensor(mo[ti % 2][:, oc, :n], Tt[0][:, :n], Tt[2][:, :n], ALU.add), reads=[Tt[0], Tt[2]], writes=[mo[ti % 2]])
                if oc == NFC - 1:
                    fw.dma(fw.sp, self.fm(self.mrgT, NFC, t0, n), mo[ti % 2][:, :, :n], mo[ti % 2], reads=[mo[ti % 2]], writes=[self.mrgT])

    def wout(self, l):
        fw, T = self.fw, self.T
        with self.phase():
            w = fw.sb("u_w", [128, NFC, D], BF16)
            wsrc = self.w_out.t.ap()[l].rearrange("(kc p) n -> p kc n", p=128)
            for q in range(8):
                fw.dma(fw.pool, w[:, :, q * 256:(q + 1) * 256], wsrc[:, :, q * 256:(q + 1) * 256], w, reads=[self.w_out], writes=[w])
            mi = [fw.sb("u_m%d" % i, [128, NFC, 512], BF16) for i in range(2)]
            xi = [fw.sb("u_x%d" % i, [128, NFC, 512], F32) for i in range(2)]
            pp = [fw.ps("u_p%d" % i, [128, 512], F32) for i in range(4)]

            def load(ti):
                t0, n, c = self.tiles[ti]
                fw.dma(fw.sp, mi[ti % 2][:, :, :n], self.fm(self.mrgT, NFC, t0, n), mi[ti % 2], reads=[self.mrgT], writes=[mi[ti % 2]])
                fw.dma(fw.sp, xi[ti % 2][:, :, :n], self.xT_tile(t0, n), xi[ti % 2], reads=[self.xT], writes=[xi[ti % 2]])

            load(0)
            it = 0
            for ti, (t0, n, c) in enumerate(self.tiles):
                m, x = mi[ti % 2], xi[ti % 2]
                if ti + 1 < len(self.tiles):
                    load(ti + 1)
                for oc in range(NFC):
                    p = pp[it % 4]
                    it += 1
                    for kc in range(NFC):
                        fw.op(fw.pe, lambda e: e.matmul(p[:, :n], w[:, kc, oc * 128:(oc + 1) * 128], m[:, kc, :n], start=(kc == 0), stop=(kc == NFC - 1)),
                              reads=[w, m], writes=[p])
                    fw.op(fw.dve, lambda e: e.scalar_tensor_tensor(x[:, oc, :n], p[:, :n], self.mT[:, 32 + oc, c:c + 1], x[:, oc, :n], ALU.mult, ALU.add),
                          reads=[p, x, self.mT], writes=[x])
                fw.dma(fw.sp, self.xT_tile(t0, n), x[:, :, :n], x, reads=[x], writes=[self.xT])

    def ffn_up(self, l, hT):
        fw, T, L = self.fw, self.T, self.L
        W = L + 259
        c_ctx, c_lat = 1, 258
        NO = L + 257
        with self.phase():
            ws = [fw.sb("f_w%d" % i, [128, NFC, 128], BF16) for i in range(3)]
            ug = fw.sb("f_ug", [128, W], BF16)
            uv = fw.sb("f_uv", [128, W], BF16)
            acc = fw.sb("f_acc", [128, W], F32)
            sgb = fw.sb("f_sg", [128, W], BF16)
            ob = fw.sb("f_o", [128, T], BF16)
            cw = fw.sb("f_cw", [128, 88, 4], F32)
            pm = [fw.ps("f_p%d" % i, [128, 512], F32) for i in range(6)]
            fw.dma(fw.sp, cw[:], self.fcw.t.ap()[l], cw, reads=[self.fcw], writes=[cw])
            fw.op(fw.pool, lambda e: e.memset(ug[:], 0.0), writes=[ug])
            fw.op(fw.pool, lambda e: e.memset(uv[:], 0.0), writes=[uv])
            wsrc = self.w_up.t.ap()[l].rearrange("(kc p) n -> p kc n", p=128)
            iw = 0
            ip = 0
            chs = [ch for i in range(44) for ch in (i, 44 + i)]

            def load_w(k):
                fw.dma(fw.pool, ws[k % 3][:], wsrc[:, :, chs[k] * 128:(chs[k] + 1) * 128], ws[k % 3], reads=[self.w_up], writes=[ws[k % 3]])

            load_w(0)
            load_w(1)
            for i in range(44):
                for part, (ch, ub) in enumerate(((i, ug), (44 + i, uv))):
                    w = ws[iw % 3]
                    if iw + 2 < len(chs):
                        load_w(iw + 2)
                    iw += 1
                    for (t0, n, c) in self.tiles:
                        p = pm[ip % 6]
                        ip += 1
                        for kc in range(NFC):
                            fw.op(fw.pe, lambda e: e.matmul(p[:, :n], w[:, kc, :], hT[:, kc, t0:t0 + n], start=(kc == 0), stop=(kc == NFC - 1)),
                                  reads=[w, hT], writes=[p])
                        col = c_ctx + t0 if c == 1 else c_lat + (t0 - CTX)
                        fw.op(fw.act, lambda e: e.copy(ub[:, col:col + n], p[:, :n]), reads=[p], writes=[ub])
                    fw.op(fw.dve, lambda e: e.tensor_scalar(acc[:, 1:1 + NO], ub[:, 0:NO], cw[:, ch, 0:1], cw[:, ch, 3:4], ALU.mult, ALU.add),
                          reads=[ub, cw], writes=[acc])
                    for k in range(1, 3):
                        fw.op(fw.dve, lambda e: e.scalar_tensor_tensor(acc[:, 1:1 + NO], ub[:, k:k + NO], cw[:, ch, k:k + 1], acc[:, 1:1 + NO], ALU.mult, ALU.add),
                              reads=[ub, cw, acc], writes=[acc])
                    if part == 0:
                        fw.op(fw.act, lambda e: e.activation(sgb[:, 1:1 + NO], acc[:, 1:1 + NO], AF.Silu), reads=[acc], writes=[sgb])
                    else:
                        fw.op(fw.pool, lambda e: e.tensor_tensor(ob[:, 0:CTX], sgb[:, c_ctx:c_ctx + CTX], acc[:, c_ctx:c_ctx + CTX], ALU.mult),
                              reads=[sgb, acc], writes=[ob])
                        fw.op(fw.pool, lambda e: e.tensor_tensor(ob[:, CTX:T], sgb[:, c_lat:c_lat + L], acc[:, c_lat:c_lat + L], ALU.mult),
                              reads=[sgb, acc], writes=[ob])
                        fw.dma(fw.sp, self.ffaT.t.ap()[i * 128:(i + 1) * 128, :], ob[:], ob, reads=[ob], writes=[self.ffaT])

    def ffn_down(self, l):
        fw, T = self.fw, self.T
        with self.phase():
            ws = [fw.sb("d_w%d" % i, [128, 44, 512], BF16) for i in range(2)]
            ai = [fw.sb("d_a%d" % i, [128, 44, 512], BF16) for i in range(2)]
            xi = [fw.sb("d_x%d" % i, [128, 4, 512], F32) for i in range(2)]
            pp = [fw.ps("d_p%d" % i, [128, 512], F32) for i in range(4)]
            wsrc = self.w_down.t.ap()[l].rearrange("(kc p) n -> p kc n", p=128)
            iters = [(og, t0, n, c) for og in range(4) for (t0, n, c) in self.tiles]

            def load_w(og):
                w = ws[og % 2]
                for q in range(4):
                    fw.dma(fw.pool, w[:, q * 11:(q + 1) * 11, :], wsrc[:, q * 11:(q + 1) * 11, og * 512:(og + 1) * 512], w, reads=[self.w_down], writes=[w])

            def load_ax(i):
                og, t0, n, c = iters[i]
                a = ai[i % 2]
                x = xi[i % 2]
                for q in range(4):
                    fw.dma(fw.sp, a[:, q * 11:(q + 1) * 11, :n], self.fm(self.ffaT, 11, t0, n, c0=q * 11), a, reads=[self.ffaT], writes=[a])
                fw.dma(fw.sp, x[:, :, :n], self.fm(self.xT, 4, t0, n, c0=og * 4), x, reads=[self.xT], writes=[x])

            load_w(0)
            load_ax(0)
            it = 0
            for i, (og, t0, n, c) in enumerate(iters):
                w, a, x = ws[og % 2], ai[i % 2], xi[i % 2]
                if i + 1 < len(iters):
                    load_ax(i + 1)
                if t0 == 0 and og + 1 < 4:
                    load_w(og + 1)
                for j in range(4):
                    oc = og * 4 + j
                    p = pp[it % 4]
                    it += 1
                    for kc in range(44):
                        fw.op(fw.pe, lambda e: e.matmul(p[:, :n], w[:, kc, j * 128:(j + 1) * 128], a[:, kc, :n], start=(kc == 0), stop=(kc == 43)),
                              reads=[w, a], writes=[p])
                    fw.op(fw.dve, lambda e: e.scalar_tensor_tensor(x[:, j, :n], p[:, :n], self.mT[:, 80 + oc, c:c + 1], x[:, j, :n], ALU.mult, ALU.add),
                          reads=[p, x, self.mT], writes=[x])
                fw.dma(fw.sp, self.fm(self.xT, 4, t0, n, c0=og * 4), x[:, :, :n], x, reads=[x], writes=[self.xT])

    def final_phase(self):
        fw, T = self.fw, self.T
        with self.phase():
            xi = fw.sb("z_x", [128, NFC, 512], F32)
            sq = fw.sb("z_sq", [128, NFC, 512], BF16)
            rt = fw.sb("z_rt", [128, 512], F32)
            rstd = fw.sb("z_rs", [128, 512], F32)
            fn = fw.sb("z_fn", [128, NFC], F32)
            hn = fw.sb("z_hn", [128, NFC, 512], F32)
            so = [fw.sb("z_o%d" % i, [128, D], F32) for i in range(2)]
            pss = fw.ps("z_ps", [128, 512], F32)
            pt = [fw.ps("z_p%d" % i, [128, 512], F32) for i in range(4)]
            fw.dma(fw.sp, fn[:], self.fnwT.t.ap(), fn, reads=[self.fnwT], writes=[fn])
            io = 0
            for (t0, n, c) in self.tiles:
                if c == 1:
                    continue
                fw.dma(fw.sp, xi[:, :, :n], self.xT_tile(t0, n), xi, reads=[self.xT], writes=[xi])
                self.rms_tile(xi, n, sq, pss, rt, rstd, D)
                for fc in range(NFC):
                    fw.op(fw.dve, lambda e: e.scalar_tensor_tensor(hn[:, fc, :n], xi[:, fc, :n], fn[:, fc:fc + 1], rstd[:, :n], ALU.mult, ALU.mult),
                          reads=[xi, rstd, fn], writes=[hn])
                for s in range(n // 128):
                    o = so[io % 2]
                    io += 1
                    for q in range(4):
                        p = pt[q]
                        for j in range(4):
                            fc = q * 4 + j
                            fw.op(fw.pe, lambda e: e.transpose(p[:, j * 128:(j + 1) * 128], hn[:, fc, s * 128:(s + 1) * 128], self.identf), reads=[hn], writes=[p])
                        if q % 2 == 0:
                            fw.op(fw.dve, lambda e: e.tensor_copy(o[:, q * 512:(q + 1) * 512], p[:]), reads=[p], writes=[o])
                        else:
                            fw.op(fw.act, lambda e: e.copy(o[:, q * 512:(q + 1) * 512], p[:]), reads=[p], writes=[o])
                    r0 = t0 - CTX + s * 128
                    fw.dma(fw.sp, self.out.t.ap()[r0:r0 + 128, :], o[:], o, reads=[o], writes=[self.out])


def _fmT(v):
    v = np.asarray(v, np.float32)
    n = v.shape[-1] // 128
    return np.ascontiguousarray(np.swapaxes(v.reshape(v.shape[:-1] + (n, 128)), -1, -2))


def _consts():
    c = np.zeros((128, 7, 128), np.float32)
    c[:, 0, :] = np.eye(128)
    c[:, 1, :] = 1.0
    rm = np.zeros((128, 128), np.float32)
    for d in range(128):
        if (d % 64) < 32:
            rm[d + 32, d] = -1.0
        else:
            rm[d - 32, d] = 1.0
    c[:, 2, :] = rm
    tri = (np.arange(128)[:, None] <= np.arange(128)[None, :]).astype(np.float32)
    c[:, 3, :] = tri
    c[:, 4, :] = tri.T
    c[:, 5, :] = (1.0 - tri) * -30000.0
    c[:, 6, :] = (1.0 - tri.T) * -30000.0
    return c


def _rope(L):
    T = CTX + L
    cos = np.ones((128, T), np.float32)
    sin = np.zeros((128, T), np.float32)
    pos = np.arange(L)
    row = (pos // 64).astype(np.float32)
    col = (pos % 64).astype(np.float32)
    inv = (10000.0 ** (-np.arange(0, 64, 2, dtype=np.float32) / 64)).astype(np.float32)
    for d in range(128):
        axis = d // 64
        f = d % 32
        ang = (row if axis == 0 else col) * inv[f]
        cos[d, CTX:] = np.cos(ang.astype(np.float32))
        sin[d, CTX:] = np.sin(ang.astype(np.float32))
    return cos, sin


def make_in_maps(inp, L, depth, batches):
    f = lambda k: np.asarray(inp[k], np.float32)
    dp = depth
    shared = {
        "w_mod": np.ascontiguousarray(f("w_mod")[:dp]),
        "b_modT": _fmT(f("b_mod")[:dp]),
        "nmixT": _fmT(f("norm_mix_w")[:dp]),
        "nffnT": _fmT(f("norm_ffn_w")[:dp]),
        "w_in": np.ascontiguousarray(f("w_in")[:dp]),
        "qkn": np.ascontiguousarray(np.stack([f("q_norm_w")[:dp], f("k_norm_w")[:dp]], axis=-1)),
        "w_attn_o": np.ascontiguousarray(f("w_attn_o")[:dp]),
        "ccw": np.ascontiguousarray(np.transpose(f("conf_conv_w")[:dp].reshape(dp, 31, 8, 128), (0, 3, 2, 1))),
        "ccb": np.ascontiguousarray(np.stack([_fmT(f("conf_conv_b")[:dp]), _fmT(f("conf_ln_w")[:dp]), _fmT(f("conf_ln_b")[:dp])], axis=-1)),
        "w_conf_o": np.ascontiguousarray(f("w_conf_o")[:dp]),
        "scw": np.ascontiguousarray(np.concatenate([np.transpose(f("ssm_conv_w")[:dp].reshape(dp, 5, 12, 128), (0, 3, 2, 1)),
                                                    _fmT(f("ssm_conv_b")[:dp])[..., None]], axis=-1)),
        "snw": _fmT(f("ssm_norm_w")[:dp]),
        "w_ssm_o": np.ascontiguousarray(f("w_ssm_o")[:dp]),
        "w_out": np.ascontiguousarray(f("w_out")[:dp]),
        "w_up": np.ascontiguousarray(f("ffn_w_up")[:dp]),
        "fcw": np.ascontiguousarray(np.concatenate([np.transpose(f("ffn_conv_w")[:dp].reshape(dp, 3, 88, 128), (0, 3, 2, 1)),
                                                    _fmT(f("ffn_conv_b")[:dp])[..., None]], axis=-1)),
        "w_down": np.ascontiguousarray(f("ffn_w_down")[:dp]),
        "fnwT": _fmT(f("final_norm_w")),
        "consts": _consts(),
    }
    srow = np.zeros((dp, 128, 4, 32), np.float32)
    srow[:, :, 0, :] = f("ssm_a_log")[:dp].reshape(dp, 1, 32)
    srow[:, :, 1, :] = f("ssm_dt_bias")[:dp].reshape(dp, 1, 32)
    srow[:, :, 2, :] = f("ssm_d")[:dp].reshape(dp, 1, 32)
    shared["srow"] = srow
    cos, sin = _rope(L)
    shared["cosT"], shared["sinT"] = cos, sin
    maps = []
    for b in batches:
        m = dict(shared)
        m["x"] = np.ascontiguousarray(f("x")[b, :L])
        m["ctx"] = np.ascontiguousarray(f("ctx")[b])
        m["cT"] = np.ascontiguousarray(np.stack([_fmT(f("c")[b]), _fmT(f("c_ctx"))], axis=-1))
        maps.append(m)
    return maps


_NC_CACHE = {}


def kernel(**inputs):
    L = 4096
    key = (L, DEPTH)
    if key not in _NC_CACHE:
        _NC_CACHE[key] = Builder(L, DEPTH).build()
    nc = _NC_CACHE[key]
    real = make_in_maps(inputs, L, DEPTH, [0, 1, 2, 3])
    work = [0, 1, 2, 3]
    maps = real + real
    res = run_bass_kernel_spmd(nc, maps, core_ids=list(range(8)))
    out = np.stack([np.asarray(res.results[cid]["out"], np.float32) for cid in work], axis=0)
    return out
```
